# Optimizing a Trainium2 kernel written in Bass

```python
import jax, jax.numpy as jnp
from jax import lax
import numpy as np

D_MODEL = 4096
BATCH = 2
SEQ = 4096
DEPTH = 1

HEAD_DIM = 128
MIX_WIDTH = D_MODEL
GDN_HEADS = MIX_WIDTH // 2 // HEAD_DIM
ATT_HEADS = MIX_WIDTH // 2 // HEAD_DIM
GDN_WIDTH = GDN_HEADS * HEAD_DIM
ATT_WIDTH = ATT_HEADS * HEAD_DIM
GDN_CONV = 3
GDN_CHUNK = 64
DILATED_PATTERNS = ((128, 1), (512, 4), (2048, 16))
ALIBI_MAX_BIAS = 8.0
FFN_DIM = 256 * ((8 * D_MODEL // 3 + 255) // 256)
FFN_CONV = 3
PLE_DIM = 256
NORM_EPS = 1e-6
NEG_INF = -1e30
IN_COLS = 3 * GDN_WIDTH + GDN_WIDTH + 4 * GDN_HEADS + 3 * ATT_WIDTH

kernel_name = "hymba_gdn_dilated_alibi_convglu_ple"


def rms_norm(x, w):
    xf = x.astype(jnp.float32)
    y = xf * lax.rsqrt(jnp.mean(xf * xf, axis=-1, keepdims=True) + NORM_EPS)
    return (y * w.astype(jnp.float32)).astype(x.dtype)


def l2_norm(x):
    return x * lax.rsqrt(jnp.sum(x * x, axis=-1, keepdims=True) + NORM_EPS)


def depthwise_conv_centred(x, w):
    k_width = w.shape[0]
    half = k_width // 2
    seq = x.shape[1]
    xp = jnp.pad(x, ((0, 0), (half, half), (0, 0)))
    w = w.astype(x.dtype)
    out = xp[:, 0:seq] * w[0]
    for j in range(1, k_width):
        out = out + xp[:, j:j + seq] * w[j]
    return out


def gated_delta_chunked(q, k, v, g, beta):
    bsz, heads, seq, dk = q.shape
    dv = v.shape[-1]
    c = GDN_CHUNK
    nc = seq // c
    q, k, v = (t.reshape(bsz, heads, nc, c, -1) for t in (q, k, v))
    g = jnp.cumsum(g.reshape(bsz, heads, nc, c), axis=-1)
    beta = beta.reshape(bsz, heads, nc, c, 1)
    incl = jnp.tril(jnp.ones((c, c), dtype=bool))
    strict = jnp.tril(jnp.ones((c, c), dtype=bool), -1)
    decay = jnp.where(incl, jnp.exp(jnp.where(incl, g[..., :, None] - g[..., None, :], 0.0)), 0.0)
    k_beta = k * beta
    lower = jnp.where(strict, jnp.einsum('bhnid,bhnjd->bhnij', k_beta, k) * decay, 0.0)
    tmat = lower + jnp.eye(c, dtype=q.dtype)
    u = lax.linalg.triangular_solve(tmat, v * beta, left_side=True, lower=True)
    w = lax.linalg.triangular_solve(tmat, k_beta * jnp.exp(g)[..., None], left_side=True, lower=True)
    intra = jnp.einsum('bhnid,bhnjd->bhnij', q, k) * decay
    q_dec = q * jnp.exp(g)[..., None]
    k_dec = k * jnp.exp(g[..., -1:] - g)[..., None]
    chunk_decay = jnp.exp(g[..., -1])
    xs = tuple(jnp.moveaxis(t, 2, 0) for t in (q_dec, k_dec, u, w, intra, chunk_decay))

    def step(state, inp):
        qd, kd, uc, wc, ic, cd = inp
        v_new = uc - jnp.einsum('bhck,bhkv->bhcv', wc, state)
        out = jnp.einsum('bhck,bhkv->bhcv', qd, state) + jnp.einsum('bhij,bhjv->bhiv', ic, v_new)
        state = state * cd[..., None, None] + jnp.einsum('bhck,bhcv->bhkv', kd, v_new)
        return state, out

    state0 = jnp.zeros((bsz, heads, dk, dv), q.dtype)
    _, out = lax.scan(step, state0, xs)
    return jnp.moveaxis(out, 0, 2).reshape(bsz, heads, seq, dv)


def gdn_mixer(qkv, z, a, b, conv_w, a_log, dt_bias, norm_w):
    bsz, seq, _ = qkv.shape
    qkv = jax.nn.silu(depthwise_conv_centred(qkv, conv_w)).astype(jnp.float32)
    q, k, v = jnp.split(qkv, 3, axis=-1)
    q, k, v = (t.reshape(bsz, seq, GDN_HEADS, HEAD_DIM).transpose(0, 2, 1, 3) for t in (q, k, v))
    q = l2_norm(q) * (HEAD_DIM ** -0.5)
    k = l2_norm(k)
    a = a.astype(jnp.float32).reshape(bsz, seq, 2, GDN_HEADS)
    b = b.astype(jnp.float32).reshape(bsz, seq, 2, GDN_HEADS)
    g = -jnp.exp(a_log.astype(jnp.float32)) * jax.nn.softplus(a + dt_bias.astype(jnp.float32))
    beta = jax.nn.sigmoid(b)
    g = g.transpose(2, 0, 3, 1)
    beta = beta.transpose(2, 0, 3, 1)
    o_fwd = gated_delta_chunked(q, k, v, g[0], beta[0])
    flip = lambda t: jnp.flip(t, axis=2)
    o_bwd = flip(gated_delta_chunked(flip(q), flip(k), flip(v), flip(g[1]), flip(beta[1])))
    o = (o_fwd + o_bwd).transpose(0, 2, 1, 3)
    o = o * lax.rsqrt(jnp.mean(o * o, axis=-1, keepdims=True) + NORM_EPS) * norm_w.astype(jnp.float32)
    o = o * jax.nn.silu(z.astype(jnp.float32).reshape(bsz, seq, GDN_HEADS, HEAD_DIM))
    return o.reshape(bsz, seq, GDN_WIDTH).astype(z.dtype)


def dilated_window_attention(q, k, v, slopes, radius, dilation):
    bsz, seq, heads, hd = q.shape
    n = seq // dilation
    blk = min(radius, n)
    nb = -(-n // blk)
    n_pad = nb * blk

    def to_blocks(t):
        t = t.reshape(bsz, n, dilation, heads, hd).transpose(0, 2, 1, 3, 4)
        t = jnp.pad(t, ((0, 0), (0, 0), (0, n_pad - n), (0, 0), (0, 0)))
        return t.reshape(bsz, dilation, nb, blk, heads, hd)

    def band(t):
        t = jnp.pad(t, ((0, 0), (0, 0), (1, 1), (0, 0), (0, 0), (0, 0)))
        return jnp.concatenate([t[:, :, :-2], t[:, :, 1:-1], t[:, :, 2:]], axis=3)

    qb = to_blocks(q)
    kn, vn = band(to_blocks(k)), band(to_blocks(v))
    s = jnp.einsum('brnqhe,brnkhe->brnhqk', qb, kn, preferred_element_type=jnp.float32)
    qi = (jnp.arange(nb)[:, None] * blk + jnp.arange(blk)[None, :])[:, :, None]
    kj = (jnp.arange(nb)[:, None] * blk - blk + jnp.arange(3 * blk)[None, :])[:, None, :]
    dist = jnp.abs(kj - qi)
    valid = (dist <= radius) & (kj >= 0) & (kj < n)
    bias = -slopes[None, :, None, None] * (dist * dilation).astype(jnp.float32)[:, None]
    s = jnp.where(valid[:, None], s + bias, NEG_INF)
    m = jnp.max(s, axis=-1, keepdims=True)
    e = jnp.exp(s - m)
    den = jnp.sum(e, axis=-1, keepdims=True)
    o = jnp.einsum('brnhqk,brnkhe->brnqhe', e / den, vn.astype(jnp.float32))
    lse = (m + jnp.log(den))[..., 0]
    o = o.reshape(bsz, dilation, n_pad, heads, hd)[:, :, :n]
    o = o.transpose(0, 2, 1, 3, 4).reshape(bsz, seq, heads, hd)
    lse = lse.transpose(0, 1, 2, 4, 3).reshape(bsz, dilation, n_pad, heads)[:, :, :n]
    lse = lse.transpose(0, 2, 1, 3).reshape(bsz, seq, heads)
    return o, lse


def dilated_mixer(qkv):
    bsz, seq, _ = qkv.shape
    q, k, v = jnp.split(qkv, 3, axis=-1)
    q, k, v = (t.reshape(bsz, seq, ATT_HEADS, HEAD_DIM) for t in (q, k, v))
    q = q * (HEAD_DIM ** -0.5)
    slopes = jnp.exp2(-ALIBI_MAX_BIAS * (jnp.arange(ATT_HEADS, dtype=jnp.float32) + 1.0) / ATT_HEADS)
    outs, lses = [], []
    for window, dilation in DILATED_PATTERNS:
        o, lse = dilated_window_attention(q, k, v, slopes, window // (2 * dilation), dilation)
        outs.append(o)
        lses.append(lse)
    wts = jax.nn.softmax(jnp.stack(lses, axis=0), axis=0)
    o = jnp.einsum('gbsh,gbshe->bshe', wts, jnp.stack(outs, axis=0))
    return o.reshape(bsz, seq, ATT_WIDTH).astype(qkv.dtype)


def setup_inputs(seed: int = 0) -> dict:
    key = jax.random.key(seed)
    ks = jax.random.split(key, 20)
    f32 = jnp.float32
    nrm = lambda k, shape, fan: jax.random.normal(k, shape, f32) * (fan ** -0.5)
    gain = lambda k, shape: 1.0 + 0.02 * jax.random.normal(k, shape, f32)
    dt = jnp.exp(jax.random.uniform(ks[5], (DEPTH, 2, GDN_HEADS), f32,
                                    minval=float(np.log(1e-3)), maxval=float(np.log(1e-1))))
    return {
        "x": jax.random.normal(ks[0], (BATCH, SEQ, D_MODEL), f32),
        "p": jax.random.normal(ks[1], (DEPTH, BATCH, SEQ, PLE_DIM), f32),
        "attn_norm": gain(ks[2], (DEPTH, D_MODEL)),
        "w_in": nrm(ks[3], (DEPTH, D_MODEL, IN_COLS), D_MODEL),
        "gdn_conv": nrm(ks[4], (DEPTH, GDN_CONV, 3 * GDN_WIDTH), GDN_CONV),
        "gdn_a_log": jnp.log(jax.random.uniform(ks[6], (DEPTH, 2, GDN_HEADS), f32, minval=1.0, maxval=16.0)),
        "gdn_dt_bias": dt + jnp.log(-jnp.expm1(-dt)),
        "gdn_out_norm": gain(ks[7], (DEPTH, HEAD_DIM)),
        "w_out": nrm(ks[8], (DEPTH, MIX_WIDTH, D_MODEL), MIX_WIDTH),
        "ffn_norm": gain(ks[9], (DEPTH, D_MODEL)),
        "w_up": nrm(ks[10], (DEPTH, D_MODEL, 2 * FFN_DIM), D_MODEL),
        "ffn_conv": nrm(ks[11], (DEPTH, FFN_CONV, 2 * FFN_DIM), FFN_CONV),
        "w_down": nrm(ks[12], (DEPTH, FFN_DIM, D_MODEL), FFN_DIM),
        "ple_norm": gain(ks[13], (DEPTH, D_MODEL)),
        "w_ple_gate": nrm(ks[14], (DEPTH, D_MODEL, D_MODEL), D_MODEL),
        "w_ple_proj": nrm(ks[15], (DEPTH, PLE_DIM, D_MODEL), PLE_DIM),
        "final_norm": gain(ks[16], (D_MODEL,)),
    }


def reference(x, p, attn_norm, w_in, gdn_conv, gdn_a_log, gdn_dt_bias, gdn_out_norm, w_out,
              ffn_norm, w_up, ffn_conv, w_down, ple_norm, w_ple_gate, w_ple_proj, final_norm):
    sizes = [3 * GDN_WIDTH, GDN_WIDTH, 2 * GDN_HEADS, 2 * GDN_HEADS, 3 * ATT_WIDTH]
    split_idx = np.cumsum(sizes)[:-1].tolist()
    h = x
    for i in range(DEPTH):
        u = rms_norm(h, attn_norm[i])
        proj = u @ w_in[i]
        gdn_qkv, gdn_z, gdn_a, gdn_b, att_qkv = jnp.split(proj, split_idx, axis=-1)
        o_gdn = gdn_mixer(gdn_qkv, gdn_z, gdn_a, gdn_b, gdn_conv[i], gdn_a_log[i],
                          gdn_dt_bias[i], gdn_out_norm[i])
        o_att = dilated_mixer(att_qkv)
        h = h + jnp.concatenate([o_gdn, o_att], axis=-1) @ w_out[i]
        u = rms_norm(h, ffn_norm[i])
        gu = depthwise_conv_centred(u @ w_up[i], ffn_conv[i])
        gate, up = jnp.split(gu, 2, axis=-1)
        h = h + (jax.nn.silu(gate) * up) @ w_down[i]
        u = rms_norm(h, ple_norm[i])
        h = h + jax.nn.sigmoid(u @ w_ple_gate[i]) * (p[i] @ w_ple_proj[i])
    return rms_norm(h, final_norm)
```

```python
import numpy as np
import concourse.bass as bass
import concourse.mybir as mybir
from concourse.bass_utils import run_bass_kernel_spmd

F32 = mybir.dt.float32
BF16 = mybir.dt.bfloat16
U8 = mybir.dt.uint8
F32R = mybir.dt.float32r
AF = mybir.ActivationFunctionType
ALU = mybir.AluOpType

ENG_NAMES = ("pe", "dve", "act", "pool", "sp")
D = 4096
S = 4096
NEG = -60000.0


class _Op:
    __slots__ = ("eng", "fn", "deps", "is_dma", "sem_key", "sig", "id")


class Prog:
    def __init__(self, nc, self_sync=("dve", "act", "pool")):
        self.nc = nc
        self.ops = []
        self.by_eng = {e: [] for e in ENG_NAMES}
        self.last_write = {}
        self.reads_since = {}
        self.dma_keys = {}
        self.self_sync = set(self_sync)
        self.global_dep = None

    def op(self, eng, fn, reads=(), writes=(), dma=False, sem_key=None):
        o = _Op()
        o.eng, o.fn, o.is_dma, o.sem_key, o.sig = eng, fn, dma, sem_key, False
        o.id = len(self.ops)
        psr = [r for r in reads if isinstance(r, tuple) and r[0] == "ps"]
        if psr:
            writes = list(writes) + psr
            reads = [r for r in reads if r not in psr]
        deps = set()
        if self.global_dep is not None:
            deps.add(self.global_dep)
        for r in reads:
            lw = self.last_write.get(r)
            if lw is not None:
                deps.add(lw)
        for w in writes:
            lw = self.last_write.get(w)
            if lw is not None:
                deps.add(lw)
            deps.update(self.reads_since.get(w, ()))
        o.deps = deps
        for r in reads:
            self.reads_since.setdefault(r, []).append(o.id)
        for w in writes:
            self.last_write[w] = o.id
            self.reads_since[w] = []
        if dma:
            if sem_key is None:
                o.sem_key = ("dma", tuple(writes)[0])
            self.dma_keys.setdefault(o.sem_key, 0)
        self.ops.append(o)
        self.by_eng[eng].append(o)
        return o.id

    def dma(self, eng, out, in_, reads=(), writes=(), sem_key=None):
        return self.op(eng, lambda e: e.dma_start(out=out, in_=in_), reads, writes, dma=True, sem_key=sem_key)

    def barrier(self):
        keys = set(self.last_write) | set(self.reads_since)
        bid = self.op("sp", lambda e: e.nop(), reads=(), writes=tuple(keys))
        self.global_dep = bid

    def emit(self):
        nc = self.nc
        ops = self.ops
        dma_issued = {}
        need = {}
        for o in ops:
            waits = {}
            for d in o.deps:
                y = ops[d]
                if y.is_dma:
                    k = ("D", y.sem_key)
                    waits[k] = max(waits.get(k, 0), dma_issued.get(y.sem_key, 0))
                else:
                    if y.eng == o.eng and y.eng not in self.self_sync:
                        continue
                    y.sig = True
                    k = ("E", y.eng)
                    waits[k] = max(waits.get(k, -1), d)
            need[o.id] = waits
            if o.is_dma:
                dma_issued[o.sem_key] = dma_issued.get(o.sem_key, 0) + 1
        sigidx = {}
        for e in ENG_NAMES:
            c = 0
            for o in self.by_eng[e]:
                if o.sig and not o.is_dma:
                    c += 1
                    sigidx[o.id] = c
        ctx = []
        esem = {}
        for e in ENG_NAMES:
            g = nc.semaphore("prog_" + e)
            esem[e] = g.__enter__()
            ctx.append(g)
        dsem = {}
        for i, k in enumerate(self.dma_keys.keys()):
            g = nc.semaphore("dsem_%d" % i)
            dsem[k] = g.__enter__()
            ctx.append(g)
        engobj = {"pe": "tensor", "dve": "vector", "act": "scalar", "pool": "gpsimd", "sp": "sync"}
        blk = nc.Block()
        block = blk.__enter__()
        for e in ENG_NAMES:
            lst = self.by_eng[e]
            if not lst:
                continue

            def body(eng, lst=lst, e=e):
                waited = {}
                for o in lst:
                    for k, v in need[o.id].items():
                        if k[0] == "D":
                            sem, val = dsem[k[1]], 16 * v
                        else:
                            sem, val = esem[k[1]], sigidx[v]
                        if waited.get(k, 0) >= val:
                            continue
                        waited[k] = val
                        eng.wait_ge(sem, val)
                    ins = o.fn(eng)
                    if o.is_dma:
                        ins.then_inc(dsem[o.sem_key], 16)
                    elif o.sig:
                        ins.then_inc(esem[e], 1)

            getattr(block, engobj[e])(body)
        blk.__exit__(None, None, None)
        for g in reversed(ctx):
            g.__exit__(None, None, None)


class Arena:
    def __init__(self, nc, nbytes):
        self.g = nc.sbuf_tensor("arena", [128, nbytes], U8)
        self.t = self.g.__enter__()
        self.n = nbytes
        self.off = 0

    def mark(self):
        return self.off

    def reset(self, m):
        self.off = m

    def alloc(self, shape, dt):
        esz = 4 if dt == F32 else 2
        n = 1
        for s in shape:
            n *= s
        nb = (n * esz + 63) // 64 * 64
        assert self.off + nb <= self.n, ("arena overflow", self.off, nb, self.n)
        v = self.t[:, self.off:self.off + n * esz].bitcast(dt)
        self.off += nb
        if len(shape) == 2:
            v = v.rearrange("p (a b) -> p a b", a=shape[0])
        elif len(shape) == 3:
            v = v.rearrange("p (a b c) -> p a b c", a=shape[0], b=shape[1])
        return v

    def close(self):
        self.g.__exit__(None, None, None)


def _consts_np():
    j = np.arange(128)[:, None]
    i = np.arange(128)[None, :]
    c = {}
    c["ident"] = (j == i).astype(np.float32)
    c["triL"] = (j <= i).astype(np.float32)
    c["triU"] = (j >= i).astype(np.float32)
    c["m_incl_f"] = np.where(i >= j, 0.0, NEG).astype(np.float32)
    c["m_incl_b"] = np.where(i <= j, 0.0, NEG).astype(np.float32)
    c["m_str_f"] = np.where(i > j, 0.0, NEG).astype(np.float32)
    c["m_str_b"] = np.where(i < j, 0.0, NEG).astype(np.float32)
    c["m_pos_f"] = np.where(j > i, 0.0, -NEG).astype(np.float32)
    c["m_pos_b"] = np.where(j < i, 0.0, -NEG).astype(np.float32)
    return c


CONST_NAMES = ["ident", "triL", "triU", "m_incl_f", "m_incl_b", "m_str_f", "m_str_b", "m_pos_f", "m_pos_b"]
import os as _os
PATTERNS = tuple(int(v) for v in _os.environ.get('PATS', '1,4,16').split(','))


def _alibi_tables(heads):
    kk = np.arange(128)[:, None]
    qq = np.arange(256)[None, :]
    dist = np.abs(kk - qq + 64)
    out = np.zeros((len(heads), 9, 128, 256), np.float32)
    for hi, h in enumerate(heads):
        slope = 2.0 ** (-8.0 * (h + 1.0) / 16.0)
        for pi, d in enumerate(PATTERNS):
            base = np.where(dist <= 64, -slope * d * dist.astype(np.float64), NEG)
            first = base.copy()
            first[:64, :] = NEG
            last = base.copy()
            last[64:, :] = NEG
            out[hi, pi * 3 + 0] = base
            out[hi, pi * 3 + 1] = first
            out[hi, pi * 3 + 2] = last
    return out


def build_l1(n_gdn=4, n_att=4, dbg=False):
    nc = bass.Bass("TRN2", target_bir_lowering=False)
    dram = lambda name, shape, dt, kind="ExternalInput": nc.dram_tensor(name, shape, dt, kind=kind).ap()
    xT = dram("xT", [16, 128, 32 * 256], F32)
    anw = dram("anw", [128, 32], F32)
    wg = dram("wg", [4, 128, 32 * 512], F32)
    wab = dram("wab", [128, 32 * 16], F32)
    wa = dram("wa", [4, 128, 32 * 384], F32)
    gconv = dram("gconv", [128, 36], F32)
    galog = dram("galog", [128, 8], F32)
    gdtb = dram("gdtb", [128, 8], F32)
    gnw = dram("gnw", [128, 1], F32)
    cst = dram("cst", [len(CONST_NAMES), 128, 128], F32)
    alibi = dram("alibi", [4, 9, 128, 256], F32)
    oT = dram("oT", [8, 128, S], BF16, kind="ExternalOutput")
    uT_d = dram("uT_d", [16, 128, 32 * 256], BF16, kind="Internal")
    dbg_o = dram("dbg_o", [128, 3 * (S + 2048)], BF16, kind="ExternalOutput") if dbg else None

    P = Prog(nc)
    A = Arena(nc, 204 * 1024)
    psg = [nc.psum_tensor("bank%d" % i, [128, 512], F32) for i in range(8)]
    bank = [g.__enter__() for g in psg]

    cf = A.alloc([len(CONST_NAMES), 128], F32)
    cb_ident = A.alloc([128], BF16)
    ones_b = A.alloc([128], BF16)
    ones_f = A.alloc([128], F32)
    anw_s = A.alloc([32], F32)
    gconv_s = A.alloc([36], F32)
    galog_s = A.alloc([8], F32)
    gdtb_s = A.alloc([8], F32)
    gnw_s = A.alloc([1], F32)
    wab_b = A.alloc([32, 16], BF16)
    g_col = A.alloc([32, 8], F32)
    lnb = A.alloc([32, 8], F32)
    beta = A.alloc([32, 8], F32)
    Gc = A.alloc([32, 8], F32)
    negG = A.alloc([32, 8], F32)
    Gb = A.alloc([32, 8], F32)
    cbe = A.alloc([32, 8], F32)
    cdl = A.alloc([32, 8], F32)
    ekd = A.alloc([32, 8], F32)
    gtmp = A.alloc([32, 8], F32)
    gtmp2 = A.alloc([32, 8], F32)
    C = {n: cf[:, i, :] for i, n in enumerate(CONST_NAMES)}

    P.dma("sp", cf, cst.rearrange("n p f -> p n f"), writes=["cf"])
    P.dma("sp", anw_s, anw, writes=["small"])
    P.dma("sp", gconv_s, gconv, writes=["small"], sem_key=("dma", "small"))
    P.dma("sp", galog_s, galog, writes=["small"], sem_key=("dma", "small"))
    P.dma("sp", gdtb_s, gdtb, writes=["small"], sem_key=("dma", "small"))
    P.dma("sp", gnw_s, gnw, writes=["small"], sem_key=("dma", "small"))
    P.dma("pool", wab_b, wab.rearrange("p (c n) -> p c n", n=16), writes=["wab"])
    P.op("dve", lambda e: e.tensor_copy(out=cb_ident, in_=C["ident"]), reads=["cf"], writes=["cbi"])
    P.op("dve", lambda e: e.memset(ones_b, 1.0), writes=["ones_b"])
    P.op("dve", lambda e: e.memset(ones_f, 1.0), writes=["ones_f"])

    wbuf = [A.alloc([32, 512], BF16)] * 2
    base_mark = A.mark()

    heads = [("g", h) for h in range(n_gdn)] + [("a", h) for h in range(n_att)]

    def load_w(hi):
        kind, h = heads[hi]
        if kind == "g":
            P.dma("pool", wbuf[hi % 2], wg[h].rearrange("p (c n) -> p c n", n=512), writes=[("w", 0)])
        else:
            P.dma("pool", wbuf[hi % 2][:, :, 0:384], wa[h].rearrange("p (c n) -> p c n", n=384), writes=[("w", 0)])

    if heads:
        load_w(0)

    utile = [A.alloc([32, 256], BF16) for _ in range(2)]
    xs = [A.alloc([32, 256], F32) for _ in range(2)]
    sq = [A.alloc([256], BF16) for _ in range(4)]
    rst = A.alloc([256], F32)
    for tt in range(16):
        xb = xs[tt % 2]
        ub = utile[tt % 2]
        P.dma("sp", xb, xT[tt].rearrange("p (c t) -> p c t", t=256), writes=[("xs", tt % 2)])
        for c in range(32):
            s_ = sq[c % 4]
            P.op("act", lambda e, s_=s_, xb=xb, c=c: e.activation(out=s_, in_=xb[:, c, :], func=AF.Square),
                 reads=[("xs", tt % 2)], writes=[("sq", c % 4)])
            P.op("pe", lambda e, s_=s_, c=c: e.matmul(bank[0][:, 0:256], lhsT=ones_b, rhs=s_, start=(c == 0), stop=(c == 31)),
                 reads=[("sq", c % 4), "ones_b"], writes=[("ps", 0)])
        P.op("act", lambda e: e.activation(out=rst, in_=bank[0][:, 0:256], func=AF.Sqrt, bias=1e-6, scale=1.0 / D),
             reads=[("ps", 0)], writes=["rst"])
        P.op("dve", lambda e: e.reciprocal(out=rst, in_=rst), reads=["rst"], writes=["rst"])
        for c in range(32):
            P.op("dve", lambda e, xb=xb, ub=ub, c=c: e.scalar_tensor_tensor(out=ub[:, c, :], in0=xb[:, c, :], scalar=anw_s[:, c:c + 1], in1=rst,
                                                                         op0=ALU.mult, op1=ALU.mult),
                 reads=[("xs", tt % 2), "rst", "small"], writes=[("ut", tt % 2, c)])
        P.dma("pool", uT_d[tt].rearrange("p (c t) -> p c t", t=256), ub, reads=[("ut", tt % 2, c) for c in range(32)], writes=["uT_d"],
              sem_key=("dma", "uT_d"))
    P.barrier()
    A.reset(base_mark)

    for hi, (kind, h) in enumerate(heads):
        wb = wbuf[0]
        wkey = ("w", 0)
        ncc = 4 if kind == "g" else 3
        if kind == "g":
            zs = A.alloc([S], F32)
            qkvn = A.alloc([3, S], F32)
            m_q = A.mark()
            pj = A.alloc([3, S + 2], F32)
            m_pj = A.mark()
            P.op("pool", lambda e, pj=pj: e.memset(pj[:, :, 0:1], 0.0), writes=["pj"])
            P.op("pool", lambda e, pj=pj: e.memset(pj[:, :, S + 1:S + 2], 0.0), writes=["pj"])
        else:
            qkv = A.alloc([3, S + 2048], BF16)
            P.op("pool", lambda e, qkv=qkv: e.memset(qkv[:, :, 0:1024], 0.0), writes=["qkv"])
            P.op("pool", lambda e, qkv=qkv: e.memset(qkv[:, :, 1024 + S:2048 + S], 0.0), writes=["qkv"])
        utile = [A.alloc([32, 256], BF16) for _ in range(2)]
        do_ab = (kind == "g" and h == 0)
        for tt in range(16):
            ub = utile[tt % 2]
            P.dma("sp", ub, uT_d[tt].rearrange("p (c t) -> p c t", t=256), reads=["uT_d"], writes=[("utl", tt % 2)])
            for cc in range(ncc):
                bk = (tt * ncc + cc) % 4

                def mm(e, ub=ub, cc=cc, bk=bk):
                    for k in range(32):
                        r = e.matmul(bank[bk][:, 0:256], lhsT=wb[:, k, cc * 128:(cc + 1) * 128], rhs=ub[:, k, :], start=(k == 0), stop=(k == 31))
                    return r
                P.op("pe", mm, reads=[wkey, ("utl", tt % 2)], writes=[("ps", bk)])
                sl = slice(tt * 256, (tt + 1) * 256)
                if kind == "g":
                    if cc < 3:
                        dst = pj[:, cc, 1 + tt * 256:1 + (tt + 1) * 256]
                        eng = "act" if cc % 2 == 0 else "dve"
                        if eng == "act":
                            P.op("act", lambda e, dst=dst, bk=bk: e.activation(out=dst, in_=bank[bk][:, 0:256], func=AF.Copy),
                                 reads=[("ps", bk)], writes=[("pj", cc, tt)])
                        else:
                            P.op("dve", lambda e, dst=dst, bk=bk: e.tensor_copy(out=dst, in_=bank[bk][:, 0:256]),
                                 reads=[("ps", bk)], writes=[("pj", cc, tt)])
                    else:
                        P.op("act", lambda e, bk=bk, sl=sl: e.activation(out=zs[:, sl], in_=bank[bk][:, 0:256], func=AF.Silu),
                             reads=[("ps", bk)], writes=[("zs", tt)])
                else:
                    dst = qkv[:, cc, 1024 + tt * 256:1024 + (tt + 1) * 256]
                    if cc == 0:
                        P.op("act", lambda e, dst=dst, bk=bk: e.activation(out=dst, in_=bank[bk][:, 0:256], func=AF.Copy, scale=128.0 ** -0.5),
                             reads=[("ps", bk)], writes=[("qkv", cc, tt)])
                    elif cc == 1:
                        P.op("dve", lambda e, dst=dst, bk=bk: e.tensor_copy(out=dst, in_=bank[bk][:, 0:256]),
                             reads=[("ps", bk)], writes=[("qkv", cc, tt)])
                    else:
                        P.op("act", lambda e, dst=dst, bk=bk: e.activation(out=dst, in_=bank[bk][:, 0:256], func=AF.Copy),
                             reads=[("ps", bk)], writes=[("qkv", cc, tt)])
            if do_ab:
                for half in range(2):
                    ch = tt * 2 + half

                    def mmab(e, ub=ub, half=half, ch=ch):
                        for k in range(32):
                            r = e.matmul(bank[4][:, ch * 16:(ch + 1) * 16], lhsT=ub[:, k, half * 128:(half + 1) * 128], rhs=wab_b[:, k, :],
                                         start=(k == 0), stop=(k == 31))
                        return r
                    P.op("pe", mmab, reads=["wab", ("utl", tt % 2)], writes=[("ps", 4)])
        if hi + 1 < len(heads):
            load_w(hi + 1)
        if do_ab:
            _gates(P, bank, C, ones_f, galog_s, gdtb_s, g_col, lnb, beta, Gc, negG, Gb, cbe, cdl, ekd, gtmp, gtmp2)
        if dbg and kind == "a" and h == 0:
            P.dma("sp", dbg_o.rearrange("p (a b) -> p a b", a=3), qkv, reads=[("qkv", t, tt) for t in range(3) for tt in range(16)] + ["qkv"], writes=["dbg_o"])
        if kind == "g" and _os.environ.get("SKIPG"):
            pass
        elif kind == "g":
            P.barrier()
            A.reset(m_pj)
            _gdn_conv(P, A, bank, ones_b, h, pj, qkvn, gconv_s)
            P.barrier()
            A.reset(m_q)
            _gdn_chunks(P, A, bank, C, ones_b, h, hi, qkvn, zs, gnw_s, g_col, lnb, beta, Gc, negG, Gb, cbe, cdl, ekd, oT)
        else:
            _att_head(P, A, bank, cb_ident, ones_b, h, hi, qkv, alibi, oT)
        P.barrier()
        A.reset(base_mark)

    P.op("sp", lambda e: e.nop(), reads=["oT", "dbg_o"])
    P.emit()
    for g in reversed(psg):
        g.__exit__(None, None, None)
    A.close()
    return nc


def _gates(P, bank, C, ones_f, galog_s, gdtb_s, g_col, lnb, beta, Gc, negG, Gb, cbe, cdl, ekd, t1, t2):
    ab = bank[4][:, 0:512].rearrange("p (c k) -> p c k", k=16)
    rd = [("ps", 4), "small"]
    bc = lambda a: a.unsqueeze(1).to_broadcast([128, 32, 8])
    P.op("dve", lambda e: e.tensor_tensor(out=t1, in0=ab[:, :, 0:8], in1=bc(gdtb_s), op=ALU.add), reads=rd, writes=["t1"])
    P.op("act", lambda e: e.activation(out=t1, in_=t1, func=AF.Exp), reads=["t1"], writes=["t1"])
    P.op("act", lambda e: e.activation(out=t1, in_=t1, func=AF.Ln, bias=1.0), reads=["t1"], writes=["t1"])
    P.op("act", lambda e: e.activation(out=t2[:, 0, :], in_=galog_s, func=AF.Exp), reads=["small"], writes=["t2"])
    P.op("dve", lambda e: e.scalar_tensor_tensor(out=g_col, in0=t1, scalar=-1.0, in1=t2[:, 0, :].unsqueeze(1).to_broadcast([128, 32, 8]),
                                                 op0=ALU.mult, op1=ALU.mult), reads=["t1", "t2"], writes=["gate"])
    P.op("act", lambda e: e.activation(out=t2, in_=ab[:, :, 8:16], func=AF.Exp, scale=-1.0), reads=rd + ["gate"], writes=["t2"])
    P.op("act", lambda e: e.activation(out=t2, in_=t2, func=AF.Ln, bias=1.0), reads=["t2"], writes=["t2"])
    P.op("dve", lambda e: e.tensor_scalar(out=lnb, in0=t2, scalar1=-1.0, scalar2=None, op0=ALU.mult), reads=["t2"], writes=["gate2"])
    P.op("act", lambda e: e.activation(out=beta, in_=t2, func=AF.Exp, scale=-1.0), reads=["t2"], writes=["gate3"])
    P.op("pe", lambda e: e.matmul(bank[5][:, 0:128].rearrange("p (c k) -> p c k", k=4), lhsT=C["triL"], rhs=g_col[:, :, 0:4], start=True, stop=True),
         reads=["gate", "cf"], writes=[("ps", 5)])
    P.op("pe", lambda e: e.matmul(bank[5][:, 128:256].rearrange("p (c k) -> p c k", k=4), lhsT=C["triU"], rhs=g_col[:, :, 4:8], start=True, stop=True),
         reads=["gate", "cf"], writes=[("ps", 5)])
    P.op("pe", lambda e: e.matmul(bank[5][:, 256:512].rearrange("p (c k) -> p c k", k=8), lhsT=ones_f, rhs=g_col, start=True, stop=True),
         reads=["gate", "ones_f"], writes=[("ps", 5)])
    P.op("dve", lambda e: e.tensor_copy(out=Gc[:, :, 0:4], in_=bank[5][:, 0:128].rearrange("p (c k) -> p c k", k=4)), reads=[("ps", 5)], writes=["Gc"])
    P.op("dve", lambda e: e.tensor_copy(out=Gc[:, :, 4:8], in_=bank[5][:, 128:256].rearrange("p (c k) -> p c k", k=4)), reads=[("ps", 5)], writes=["Gc"])
    P.op("dve", lambda e: e.tensor_scalar(out=negG, in0=Gc, scalar1=-1.0, scalar2=None, op0=ALU.mult), reads=["Gc"], writes=["negG"])
    P.op("dve", lambda e: e.tensor_tensor(out=Gb, in0=Gc, in1=lnb, op=ALU.add), reads=["Gc", "gate2"], writes=["Gb"])
    P.op("act", lambda e: e.activation(out=cbe, in_=Gb, func=AF.Exp), reads=["Gb"], writes=["cbe"])
    P.op("act", lambda e: e.activation(out=cdl, in_=bank[5][:, 256:512].rearrange("p (c k) -> p c k", k=8), func=AF.Exp), reads=[("ps", 5)], writes=["cdl"])
    P.op("dve", lambda e: e.tensor_tensor(out=t1, in0=bank[5][:, 256:512].rearrange("p (c k) -> p c k", k=8), in1=Gc, op=ALU.subtract),
         reads=[("ps", 5), "Gc", "t1"], writes=["t1"])
    P.op("act", lambda e: e.activation(out=ekd, in_=t1, func=AF.Exp), reads=["t1"], writes=["ekd"])


def _gdn_conv(P, A, bank, ones_b, h, pj, qkvn, gconv_s):
    acc = [A.alloc([512], F32) for _ in range(2)]
    sqb = [A.alloc([512], BF16) for _ in range(2)]
    rs = [A.alloc([512], F32) for _ in range(2)]
    it = 0
    for t in range(3):
        for tl in range(8):
            a_ = acc[it % 2]
            s0 = tl * 512
            cw = lambda j, t=t: gconv_s[:, h * 9 + t * 3 + j:h * 9 + t * 3 + j + 1]
            ak = ("acc", it % 2)
            P.op("dve", lambda e, a_=a_, s0=s0, cw=cw, t=t: e.tensor_scalar(out=a_, in0=pj[:, t, s0:s0 + 512], scalar1=cw(0), scalar2=None, op0=ALU.mult),
                 reads=["small"], writes=[ak])
            P.op("dve", lambda e, a_=a_, s0=s0, cw=cw, t=t: e.scalar_tensor_tensor(out=a_, in0=pj[:, t, s0 + 1:s0 + 513], scalar=cw(1), in1=a_, op0=ALU.mult, op1=ALU.add),
                 reads=[ak], writes=[ak])
            P.op("dve", lambda e, a_=a_, s0=s0, cw=cw, t=t: e.scalar_tensor_tensor(out=a_, in0=pj[:, t, s0 + 2:s0 + 514], scalar=cw(2), in1=a_, op0=ALU.mult, op1=ALU.add),
                 reads=[ak], writes=[ak])
            if t == 2:
                P.op("act", lambda e, a_=a_, s0=s0: e.activation(out=qkvn[:, 2, s0:s0 + 512], in_=a_, func=AF.Silu), reads=[ak], writes=[("qkvn", 2, tl)])
            else:
                P.op("act", lambda e, a_=a_: e.activation(out=a_, in_=a_, func=AF.Silu), reads=[ak], writes=[ak])
                sb_ = sqb[it % 2]
                r_ = rs[it % 2]
                P.op("act", lambda e, a_=a_, sb_=sb_: e.activation(out=sb_, in_=a_, func=AF.Square), reads=[ak], writes=[("sqb", it % 2)])
                P.op("pe", lambda e, sb_=sb_: e.matmul(bank[6][:, 0:512], lhsT=ones_b, rhs=sb_, start=True, stop=True),
                     reads=[("sqb", it % 2), "ones_b"], writes=[("ps", 6)])
                P.op("act", lambda e, r_=r_: e.activation(out=r_, in_=bank[6][:, 0:512], func=AF.Sqrt, bias=1e-6), reads=[("ps", 6)], writes=[("rs", it % 2)])
                P.op("dve", lambda e, r_=r_: e.reciprocal(out=r_, in_=r_), reads=[("rs", it % 2)], writes=[("rs", it % 2)])
                sc = 128.0 ** -0.5 if t == 0 else 1.0
                P.op("dve", lambda e, a_=a_, r_=r_, s0=s0, t=t, sc=sc: e.scalar_tensor_tensor(out=qkvn[:, t, s0:s0 + 512], in0=a_, scalar=sc, in1=r_, op0=ALU.mult, op1=ALU.mult),
                     reads=[ak, ("rs", it % 2)], writes=[("qkvn", t, tl)])
            it += 1


def _gdn_chunks(P, A, bank, C, ones_b, h, hi, qkvn, zs, gnw_s, g_col, lnb, beta, Gc, negG, Gb, cbe, cdl, ekd, oT):
    gate_keys = ["gate", "gate2", "gate3", "Gc", "negG", "Gb", "cbe", "cdl", "ekd"]
    oacc = A.alloc([S], F32)
    obf = A.alloc([S], BF16)
    acc = [A.alloc([512], F32) for _ in range(2)]
    sqb = [A.alloc([512], BF16) for _ in range(2)]
    rs = [A.alloc([512], F32) for _ in range(2)]
    P.op("pool", lambda e: e.memset(oacc, 0.0), writes=["oacc"])
    NSL = 2
    sl = {}
    names = ["kbg", "kd", "vb", "ktm", "vtm", "E1", "E2", "E3", "Eq", "M0", "M1", "MT0", "MT1", "Pf", "IT", "qd", "nWT", "vnew"]
    NEUB = _os.environ.get("NEU_BF16", "0") == "1"
    ACTS = _os.environ.get("ACT_SCALE", "1") == "1"
    for d in range(2):
        for s_ in range(NSL):
            sl[(d, s_)] = {n: A.alloc([128], F32) for n in names}
            for n in ("Mb0", "Mb1", "MTb0", "MTb1", "Pb0", "Pb1"):
                sl[(d, s_)][n] = A.alloc([128], BF16)
    Sf = [A.alloc([128], F32) for _ in range(2)]
    for d in range(2):
        P.op("pool", lambda e, d=d: e.memset(Sf[d], 0.0), writes=[("Sf", d)])
    ident = C["ident"]
    rr = (lambda a: a.bitcast(F32R)) if _os.environ.get("F32R", "0") == "1" else (lambda a: a)

    def chunk(d, step):
        c = step if d == 0 else 31 - step
        s_ = step % NSL
        B = sl[(d, s_)]
        K = lambda n: (n, d, s_)
        gi = d * 4 + h
        cs = slice(c * 128, (c + 1) * 128)
        qT = qkvn[:, 0, cs]
        kT = qkvn[:, 1, cs]
        vT = qkvn[:, 2, cs]
        qk_r = [("qkvn", t, c // 4) for t in range(3)]
        bA = bank[0 + d]
        bB = bank[2 + 2 * d + step % 2]
        bC = bank[6 + d]
        kA, kB, kC = ("ps", 0 + d), ("ps", 2 + 2 * d + step % 2), ("ps", 6 + d)
        tri = C["triL"] if d == 0 else C["triU"]
        m_incl = C["m_incl_f"] if d == 0 else C["m_incl_b"]
        m_str = C["m_str_f"] if d == 0 else C["m_str_b"]
        m_pos = C["m_pos_f"] if d == 0 else C["m_pos_b"]
        col = lambda arr: arr[:, c, gi:gi + 1]
        gbc = col(g_col).to_broadcast([128, 128])
        lbc = col(lnb).to_broadcast([128, 128])
        P.op("pe", lambda e: e.transpose(bC[:, 0:128], kT, ident), reads=qk_r + ["cf"], writes=[kC])
        P.op("pe", lambda e: e.transpose(bC[:, 128:256], vT, ident), reads=qk_r + ["cf"], writes=[kC])
        if ACTS:
            P.op("act", lambda e: e.activation(out=B["kbg"], in_=bC[:, 0:128], func=AF.Copy, scale=col(cbe)), reads=[kC] + gate_keys, writes=[K("kbg")])
            P.op("act", lambda e: e.activation(out=B["kd"], in_=bC[:, 0:128], func=AF.Copy, scale=col(ekd)), reads=[kC] + gate_keys, writes=[K("kd")])
            P.op("act", lambda e: e.activation(out=B["vb"], in_=bC[:, 128:256], func=AF.Copy, scale=col(beta)), reads=[kC] + gate_keys, writes=[K("vb")])
        else:
            P.op("dve", lambda e: e.tensor_copy(out=B["ktm"], in_=bC[:, 0:128]), reads=[kC], writes=[K("ktm")])
            P.op("act", lambda e: e.activation(out=B["vtm"], in_=bC[:, 128:256], func=AF.Copy), reads=[kC], writes=[K("vtm")])
            P.op("dve", lambda e: e.tensor_scalar(out=B["kbg"], in0=B["ktm"], scalar1=col(cbe), scalar2=None, op0=ALU.mult), reads=[K("ktm")] + gate_keys, writes=[K("kbg")])
            P.op("pool", lambda e: e.tensor_scalar(out=B["kd"], in0=B["ktm"], scalar1=col(ekd), scalar2=None, op0=ALU.mult), reads=[K("ktm")] + gate_keys, writes=[K("kd")])
            P.op("dve", lambda e: e.tensor_scalar(out=B["vb"], in0=B["vtm"], scalar1=col(beta), scalar2=None, op0=ALU.mult), reads=[K("vtm")] + gate_keys, writes=[K("vb")])
        yield

        def mmA(e):
            e.matmul(bA[:, 0:128], lhsT=kT, rhs=kT, start=True, stop=True)
            e.matmul(bA[:, 128:256], lhsT=kT, rhs=qT, start=True, stop=True)
            e.matmul(bA[:, 256:384], lhsT=gbc, rhs=tri, start=True, stop=True)
            e.matmul(bA[:, 384:512], lhsT=gbc, rhs=tri, start=True, stop=False)
            return e.matmul(bA[:, 384:512], lhsT=ident, rhs=m_incl, start=False, stop=True)
        P.op("pe", mmA, reads=qk_r + gate_keys + ["cf"], writes=[kA])

        def mmB(e):
            e.matmul(bB[:, 0:128], lhsT=gbc, rhs=tri, start=True, stop=False)
            e.matmul(bB[:, 0:128], lhsT=lbc, rhs=ident, start=False, stop=False)
            e.matmul(bB[:, 0:128], lhsT=ident, rhs=m_str, start=False, stop=True)
            e.matmul(bB[:, 128:256], lhsT=gbc, rhs=tri, start=True, stop=False)
            return e.matmul(bB[:, 128:256], lhsT=ident, rhs=m_pos, start=False, stop=True)
        P.op("pe", mmB, reads=gate_keys + ["cf"], writes=[kB])
        yield
        P.op("act", lambda e: e.activation(out=B["Eq"], in_=bA[:, 256:384], func=AF.Exp), reads=[kA], writes=[K("Eq")])
        P.op("act", lambda e: e.activation(out=B["E1"], in_=bA[:, 384:512], func=AF.Exp, bias=col(negG)), reads=[kA] + gate_keys, writes=[K("E1")])
        P.op("act", lambda e: e.activation(out=B["E2"], in_=bB[:, 0:128], func=AF.Exp, bias=col(negG)), reads=[kB] + gate_keys, writes=[K("E2")])
        P.op("act", lambda e: e.activation(out=B["E3"], in_=bB[:, 128:256], func=AF.Exp, bias=col(Gb), scale=-1.0), reads=[kB] + gate_keys, writes=[K("E3")])
        yield
        if NEUB:
            P.op("dve", lambda e: e.tensor_tensor(out=B["Mb0"], in0=bA[:, 0:128], in1=B["E2"], op=ALU.mult), reads=[kA, K("E2")], writes=[K("Mb0")])
            P.op("dve", lambda e: e.tensor_tensor(out=B["MTb0"], in0=bA[:, 0:128], in1=B["E3"], op=ALU.mult), reads=[kA, K("E3")], writes=[K("MTb0")])
            P.op("dve", lambda e: e.tensor_tensor(out=B["IT"], in0=bA[:, 128:256], in1=B["E1"], op=ALU.mult), reads=[kA, K("E1")], writes=[K("IT")])
            P.op("dve", lambda e: e.tensor_tensor(out=B["Pf"], in0=ident, in1=B["Mb0"], op=ALU.subtract), reads=[K("Mb0"), "cf"], writes=[K("Pf")])
            P.op("act", lambda e: e.activation(out=B["Pb0"], in_=B["Pf"], func=AF.Copy), reads=[K("Pf")], writes=[K("Pb0")])
            P.op("pool", lambda e: e.tensor_tensor(out=B["qd"], in0=qT, in1=B["Eq"], op=ALU.mult), reads=qk_r + [K("Eq")], writes=[K("qd")])
            yield
            for k in range(1, 7):
                pi, ci = (k - 1) % 2, k % 2
                Mp, MTp = B["Mb%d" % pi], B["MTb%d" % pi]
                Mc, MTc = B["Mb%d" % ci], B["MTb%d" % ci]
                kMp, kMTp, kMc, kMTc = K("Mb%d" % pi), K("MTb%d" % pi), K("Mb%d" % ci), K("MTb%d" % ci)
                P.op("pe", lambda e, Mp=Mp, MTp=MTp: e.matmul(bB[:, 256:384], lhsT=Mp, rhs=MTp, start=True, stop=True), reads=[kMp, kMTp], writes=[kB])
                if k < 6:
                    P.op("pe", lambda e, Mp=Mp, MTp=MTp: e.matmul(bB[:, 384:512], lhsT=MTp, rhs=Mp, start=True, stop=True), reads=[kMp, kMTp], writes=[kB])
                P.op("act", lambda e, MTc=MTc: e.activation(out=MTc, in_=bB[:, 256:384], func=AF.Copy), reads=[kB], writes=[kMTc])
                if k < 6:
                    P.op("dve", lambda e, Mc=Mc: e.tensor_copy(out=Mc, in_=bB[:, 384:512]), reads=[kB], writes=[kMc])
                yield
                Pbp, Pbc = B["Pb%d" % pi], B["Pb%d" % ci]
                P.op("pe", lambda e, MTc=MTc, Pbp=Pbp: e.matmul(bB[:, 256:384], lhsT=MTc, rhs=Pbp, start=True, stop=True), reads=[kMTc, K("Pb%d" % pi)], writes=[kB])
                P.op("dve", lambda e: e.tensor_tensor(out=B["Pf"], in0=bB[:, 256:384], in1=B["Pf"], op=ALU.add), reads=[kB, K("Pf")], writes=[K("Pf")])
                if k < 6:
                    P.op("act", lambda e, Pbc=Pbc: e.activation(out=Pbc, in_=B["Pf"], func=AF.Copy), reads=[K("Pf")], writes=[K("Pb%d" % ci)])
                yield
        else:
            P.op("dve", lambda e: e.tensor_tensor(out=B["M0"], in0=bA[:, 0:128], in1=B["E2"], op=ALU.mult), reads=[kA, K("E2")], writes=[K("M0")])
            P.op("dve", lambda e: e.tensor_tensor(out=B["MT0"], in0=bA[:, 0:128], in1=B["E3"], op=ALU.mult), reads=[kA, K("E3")], writes=[K("MT0")])
            P.op("dve", lambda e: e.tensor_tensor(out=B["IT"], in0=bA[:, 128:256], in1=B["E1"], op=ALU.mult), reads=[kA, K("E1")], writes=[K("IT")])
            P.op("dve", lambda e: e.tensor_tensor(out=B["Pf"], in0=ident, in1=B["M0"], op=ALU.subtract), reads=[K("M0"), "cf"], writes=[K("Pf")])
            P.op("pool", lambda e: e.tensor_tensor(out=B["qd"], in0=qT, in1=B["Eq"], op=ALU.mult), reads=qk_r + [K("Eq")], writes=[K("qd")])
            yield
            for k in range(1, 7):
                pi, ci = (k - 1) % 2, k % 2
                Mp, MTp = B["M%d" % pi], B["MT%d" % pi]
                Mc, MTc = B["M%d" % ci], B["MT%d" % ci]
                kMp, kMTp, kMc, kMTc = K("M%d" % pi), K("MT%d" % pi), K("M%d" % ci), K("MT%d" % ci)
                P.op("pe", lambda e, Mp=Mp, MTp=MTp: e.matmul(bB[:, 256:384], lhsT=rr(Mp), rhs=rr(MTp), start=True, stop=True), reads=[kMp, kMTp], writes=[kB])
                if k < 6:
                    P.op("pe", lambda e, Mp=Mp, MTp=MTp: e.matmul(bB[:, 384:512], lhsT=rr(MTp), rhs=rr(Mp), start=True, stop=True), reads=[kMp, kMTp], writes=[kB])
                P.op("act", lambda e, MTc=MTc: e.activation(out=MTc, in_=bB[:, 256:384], func=AF.Copy), reads=[kB], writes=[kMTc])
                if k < 6:
                    P.op("dve", lambda e, Mc=Mc: e.tensor_copy(out=Mc, in_=bB[:, 384:512]), reads=[kB], writes=[kMc])
                yield
                P.op("pe", lambda e, MTc=MTc: e.matmul(bB[:, 256:384], lhsT=rr(MTc), rhs=rr(B["Pf"]), start=True, stop=True), reads=[kMTc, K("Pf")], writes=[kB])
                P.op("dve", lambda e: e.tensor_tensor(out=B["Pf"], in0=bB[:, 256:384], in1=B["Pf"], op=ALU.add), reads=[kB, K("Pf")], writes=[K("Pf")])
                yield
        kPf = K("Pf")
        P.op("pe", lambda e: e.matmul(bB[:, 384:512], lhsT=B["kbg"], rhs=B["Pf"], start=True, stop=True), reads=[K("kbg"), kPf], writes=[kB])
        P.op("dve", lambda e: e.tensor_scalar(out=B["nWT"], in0=bB[:, 384:512], scalar1=-1.0, scalar2=None, op0=ALU.mult), reads=[kB], writes=[K("nWT")])
        yield
        def mmv(e):
            e.matmul(bC[:, 128:256], lhsT=B["Pf"], rhs=B["vb"], start=True, stop=False)
            return e.matmul(bC[:, 128:256], lhsT=B["nWT"], rhs=Sf[d], start=False, stop=True)
        P.op("pe", mmv, reads=[kPf, K("vb"), K("nWT"), ("Sf", d)], writes=[kC])
        P.op("act", lambda e: e.activation(out=B["vnew"], in_=bC[:, 128:256], func=AF.Copy), reads=[kC], writes=[K("vnew")])
        yield

        def mmo(e):
            e.matmul(bC[:, 256:384], lhsT=Sf[d], rhs=B["qd"], start=True, stop=False)
            e.matmul(bC[:, 256:384], lhsT=B["vnew"], rhs=B["IT"], start=False, stop=True)
            return e.matmul(bC[:, 0:128], lhsT=B["kd"], rhs=B["vnew"], start=True, stop=True)
        P.op("pe", mmo, reads=[("Sf", d), K("qd"), K("vnew"), K("IT"), K("kd")], writes=[kC])
        P.op("dve", lambda e: e.scalar_tensor_tensor(out=Sf[d], in0=Sf[d], scalar=col(cdl), in1=bC[:, 0:128], op0=ALU.mult, op1=ALU.add),
             reads=[kC, ("Sf", d)] + gate_keys, writes=[("Sf", d)])
        P.op("dve", lambda e: e.tensor_tensor(out=oacc[:, cs], in0=bC[:, 256:384], in1=oacc[:, cs], op=ALU.add), reads=[kC, ("oacc", c), "oacc"], writes=[("oacc", c)])
        yield

    nsteps = int(_os.environ.get("GSTEPS", "32"))
    stag = int(_os.environ.get("GSTAG", "10"))
    active, nxt, rnd = [], 0, 0
    while nxt < nsteps or active:
        if nxt < nsteps and rnd >= nxt * stag:
            active += [chunk(0, nxt), chunk(1, nxt)]
            nxt += 1
        still = []
        for g_ in active:
            try:
                next(g_)
                still.append(g_)
            except StopIteration:
                pass
        active = still
        rnd += 1

    for tl in range(8):
        s0 = tl * 512
        a_, sb_, r_ = acc[tl % 2], sqb[tl % 2], rs[tl % 2]
        ok = [("oacc", c) for c in range(tl * 4, tl * 4 + 4)] + ["oacc"]
        P.op("act", lambda e, sb_=sb_, s0=s0: e.activation(out=sb_, in_=oacc[:, s0:s0 + 512], func=AF.Square), reads=ok, writes=[("sqb", tl % 2)])
        P.op("pe", lambda e, sb_=sb_: e.matmul(bank[6][:, 0:512], lhsT=ones_b, rhs=sb_, start=True, stop=True), reads=[("sqb", tl % 2), "ones_b"], writes=[("ps", 6)])
        P.op("act", lambda e, r_=r_: e.activation(out=r_, in_=bank[6][:, 0:512], func=AF.Sqrt, bias=1e-6, scale=1.0 / 128), reads=[("ps", 6)], writes=[("rs", tl % 2)])
        P.op("dve", lambda e, r_=r_: e.reciprocal(out=r_, in_=r_), reads=[("rs", tl % 2)], writes=[("rs", tl % 2)])
        P.op("dve", lambda e, a_=a_, r_=r_, s0=s0: e.scalar_tensor_tensor(out=a_, in0=oacc[:, s0:s0 + 512], scalar=gnw_s[:, 0:1], in1=r_, op0=ALU.mult, op1=ALU.mult),
             reads=ok + [("rs", tl % 2), "small"], writes=[("acc", tl % 2)])
        P.op("dve", lambda e, a_=a_, s0=s0: e.tensor_tensor(out=obf[:, s0:s0 + 512], in0=a_, in1=zs[:, s0:s0 + 512], op=ALU.mult),
             reads=[("acc", tl % 2)], writes=[("ofin", tl)])
    P.dma("sp", oT[hi], obf, reads=[("ofin", tl) for tl in range(8)], writes=["oT"], sem_key=("dma", "oT"))


def _att_head(P, A, bank, cb_ident, ones_b, h, hi, qkv, alibi, oT):
    tab = A.alloc([9, 256], F32)
    em = tab
    Oacc = A.alloc([S], F32)
    Dacc = A.alloc([S], F32)
    ex = [A.alloc([256], F32) for _ in range(2)]
    pt = [A.alloc([256], BF16) for _ in range(2)]
    vb = [A.alloc([128], BF16) for _ in range(2)]
    obf = qkv[:, 0, 1024:1024 + S]
    P.dma("sp", tab, alibi[h].rearrange("n p f -> p n f"), writes=["tab"])
    P.op("act", lambda e: e.activation(out=em, in_=tab, func=AF.Exp), reads=["tab"], writes=["em", "tab"])
    P.op("pool", lambda e: e.memset(Oacc, 0.0), writes=["Oacc"])
    P.op("pool", lambda e: e.memset(Dacc, 0.0), writes=["Dacc"])
    qk_all = [("qkv", t, tt) for t in range(3) for tt in range(16)] + ["qkv"]
    it = 0
    for pi, d in enumerate(PATTERNS):
        n = S // d
        nb = n // 128
        for r in range(d):
            for j in range(nb + 1):
                var = 1 if j == 0 else (2 if j == nb else 0)
                q0, q1 = (128, 256) if j == 0 else ((0, 128) if j == nb else (0, 256))
                nq = q1 - q0
                kc0 = 1024 + r + d * (128 * j - 64)
                ks = slice(kc0, kc0 + 127 * d + 1, d)
                qc0 = 1024 + r + d * (128 * j - 128 + q0)
                qs = slice(qc0, qc0 + (nq - 1) * d + 1, d)
                sb_ = it % 2
                bS = bank[sb_]
                bT = bank[2 + sb_]
                P.op("pe", lambda e, bS=bS, ks=ks, qs=qs, nq=nq: e.matmul(bS[:, 0:nq], lhsT=qkv[:, 1, ks], rhs=qkv[:, 0, qs], start=True, stop=True),
                     reads=qk_all, writes=[("ps", sb_)])
                tbv = bT[:, 0:64].bitcast(BF16)
                P.op("pe", lambda e, tbv=tbv, ks=ks: e.transpose(tbv, qkv[:, 2, ks], cb_ident), reads=qk_all + ["cbi"], writes=[("ps", 2 + sb_)])
                P.op("act", lambda e, bS=bS, nq=nq, sb_=sb_: e.activation(out=ex[sb_][:, 0:nq], in_=bS[:, 0:nq], func=AF.Exp), reads=[("ps", sb_)], writes=[("ex", sb_)])
                P.op("dve", lambda e, tbv=tbv, sb_=sb_: e.tensor_copy(out=vb[sb_], in_=tbv), reads=[("ps", 2 + sb_)], writes=[("vb", sb_)])
                P.op("dve", lambda e, sb_=sb_, nq=nq, q0=q0, q1=q1, pi=pi, var=var: e.tensor_tensor(out=pt[sb_][:, 0:nq], in0=ex[sb_][:, 0:nq], in1=em[:, pi * 3 + var, q0:q1], op=ALU.mult),
                     reads=[("ex", sb_), "em"], writes=[("pt", sb_)])
                for half in range(2):
                    lo = half * 128
                    if lo < q0 or lo >= q1:
                        continue
                    tq = j - 1 + half
                    first = (half == 1) or (tq == 0 and False)
                    pb = 4 + (tq % 2)
                    pcol = slice(lo - q0, lo - q0 + 128)
                    is_first = (half == 1)
                    is_last = (half == 0)

                    def mmpv(e, pb=pb, sb_=sb_, pcol=pcol, is_first=is_first, is_last=is_last):
                        e.matmul(bank[pb][:, 0:128], lhsT=vb[sb_], rhs=pt[sb_][:, pcol], start=is_first, stop=is_last)
                        return e.matmul(bank[pb + 2][:, 0:128], lhsT=ones_b, rhs=pt[sb_][:, pcol], start=is_first, stop=is_last)
                    P.op("pe", mmpv, reads=[("vb", sb_), ("pt", sb_), "ones_b"], writes=[("ps", pb), ("ps", pb + 2)])
                    if is_last:
                        oc0 = r + d * 128 * tq
                        osl = slice(oc0, oc0 + 127 * d + 1, d)
                        P.op("dve", lambda e, pb=pb, osl=osl: e.tensor_tensor(out=Oacc[:, osl], in0=bank[pb][:, 0:128], in1=Oacc[:, osl], op=ALU.add),
                             reads=[("ps", pb), "Oacc"], writes=["Oacc"])
                        P.op("dve", lambda e, pb=pb, osl=osl: e.tensor_tensor(out=Dacc[:, osl], in0=bank[pb + 2][:, 0:128], in1=Dacc[:, osl], op=ALU.add),
                             reads=[("ps", pb + 2), "Dacc"], writes=["Dacc"])
                it += 1
    for tl in range(8):
        s0 = tl * 512
        P.op("dve", lambda e, s0=s0: e.reciprocal(out=Dacc[:, s0:s0 + 512], in_=Dacc[:, s0:s0 + 512]), reads=["Dacc"], writes=["Dacc"])
        P.op("dve", lambda e, s0=s0: e.tensor_tensor(out=obf[:, s0:s0 + 512], in0=Oacc[:, s0:s0 + 512], in1=Dacc[:, s0:s0 + 512], op=ALU.mult),
             reads=["Dacc", "Oacc"], writes=["obf", "qkv"])
    P.dma("sp", oT[hi], obf, reads=["obf"], writes=["oT"], sem_key=("dma", "oT"))


def _ptile(w):
    K, N = w.shape
    return np.ascontiguousarray(w.reshape(K // 128, 128, N).transpose(1, 0, 2)).reshape(128, (K // 128) * N)


def l1_inputs(inp, core):
    b, g = core // 4, core % 4
    x = inp["x"][b]
    w_in = inp["w_in"][0]
    hs = [4 * g + i for i in range(4)]
    xT = np.ascontiguousarray(x.T.reshape(32, 128, 16, 256).transpose(2, 1, 0, 3)).reshape(16, 128, 32 * 256)
    wg = np.stack([_ptile(np.concatenate([w_in[:, 0 + h * 128:0 + (h + 1) * 128], w_in[:, 2048 + h * 128:2048 + (h + 1) * 128],
                                          w_in[:, 4096 + h * 128:4096 + (h + 1) * 128], w_in[:, 6144 + h * 128:6144 + (h + 1) * 128]], axis=1)) for h in hs])
    acols = [8192 + d * 16 + h for d in range(2) for h in hs] + [8224 + d * 16 + h for d in range(2) for h in hs]
    wab = _ptile(w_in[:, acols])
    A0 = 8256
    wa = np.stack([_ptile(np.concatenate([w_in[:, A0 + h * 128:A0 + (h + 1) * 128], w_in[:, A0 + 2048 + h * 128:A0 + 2048 + (h + 1) * 128],
                                          w_in[:, A0 + 4096 + h * 128:A0 + 4096 + (h + 1) * 128]], axis=1)) for h in hs])
    gc = inp["gdn_conv"][0]
    gconv = np.zeros((128, 36), np.float32)
    for hi, h in enumerate(hs):
        for t in range(3):
            for j in range(3):
                gconv[:, hi * 9 + t * 3 + j] = gc[j, t * 2048 + h * 128:t * 2048 + (h + 1) * 128]
    sel = lambda a: np.ascontiguousarray(np.broadcast_to(np.array([a[d, h] for d in range(2) for h in hs], np.float32)[None, :], (128, 8)))
    cn = _consts_np()
    return {
        "xT": xT, "anw": np.ascontiguousarray(inp["attn_norm"][0].reshape(32, 128).T), "wg": wg, "wab": wab, "wa": wa,
        "gconv": gconv, "galog": sel(inp["gdn_a_log"][0]), "gdtb": sel(inp["gdn_dt_bias"][0]),
        "gnw": np.ascontiguousarray(inp["gdn_out_norm"][0].reshape(128, 1)),
        "cst": np.stack([cn[n] for n in CONST_NAMES]), "alibi": _alibi_tables(hs),
    }


def run_l1(inputs):
    nc = build_l1(4, 4)
    in_maps = [l1_inputs(inputs, c) for c in range(8)]
    res = run_bass_kernel_spmd(nc, in_maps, core_ids=list(range(8)))
    oT = np.zeros((2, 4096, S), dtype=np.asarray(res.results[0]["oT"]).dtype)
    for c in range(8):
        b, g = c // 4, c % 4
        o = np.asarray(res.results[c]["oT"])
        for i in range(4):
            h = 4 * g + i
            oT[b, h * 128:(h + 1) * 128] = o[i]
            oT[b, 2048 + h * 128:2048 + (h + 1) * 128] = o[4 + i]
    return oT


NW = 514
NI = 512
NFC = 86
FG = 2


def build_l2(n_win=2, stop_after=None):
    nc = bass.Bass("TRN2", target_bir_lowering=False)
    dram = lambda name, shape, dt, kind="ExternalInput": nc.dram_tensor(name, shape, dt, kind=kind).ap()
    xw = dram("xw", [2, 128, 32 * NW], F32)
    ow = dram("ow", [2, 128, 32 * NW], BF16)
    pw = dram("pw", [2, 128, 2 * NI], F32)
    wo = dram("wo", [32, 128, 32 * 128], F32)
    wu = dram("wu", [NFC, 128, 32 * 256], F32)
    wd = dram("wd", [NFC, 128, D], F32)
    wgt = dram("wgt", [32, 128, 32 * 128], F32)
    wp = dram("wp", [128, 2 * D], F32)
    nrm = dram("nrm", [128, 96], F32)
    fcv = dram("fcv", [128, NFC * 6], F32)
    outT = dram("outT", [2, 32, 128, NI], F32, kind="ExternalOutput")

    P = Prog(nc)
    A = Arena(nc, 204 * 1024)
    psg = [nc.psum_tensor("bank%d" % i, [128, 512], F32) for i in range(8)]
    bank = [g.__enter__() for g in psg]

    hT = A.alloc([32, NW], F32)
    uT = A.alloc([32, NW], BF16)
    ones_b = A.alloc([128], BF16)
    nrm_s = A.alloc([96], F32)
    fcv_s = A.alloc([NFC * 6], F32)
    rstd = A.alloc([NW], F32)
    sq = [A.alloc([NW], BF16) for _ in range(2)]
    zero_f = A.alloc([256], F32)
    base_mark = A.mark()
    P.dma("sp", nrm_s, nrm, writes=["consts"])
    P.dma("sp", fcv_s, fcv, writes=["consts"], sem_key=("dma", "consts"))
    P.op("dve", lambda e: e.memset(ones_b, 1.0), writes=["ones_b"])
    P.op("dve", lambda e: e.memset(zero_f, 0.0), writes=["zero_f"])
    hkeys = [("h", c) for c in range(32)]
    ukeys = [("u", c) for c in range(32)]

    def norm(ncol, widx, final_w=None):
        for c in range(32):
            s_ = sq[c % 2]
            P.op("act", lambda e, s_=s_, c=c: e.activation(out=s_, in_=hT[:, c, :], func=AF.Square), reads=[("h", c)], writes=[("sq", c % 2)])

            def mmn(e, s_=s_, c=c):
                e.matmul(bank[6][:, 0:257], lhsT=ones_b, rhs=s_[:, 0:257], start=(c == 0), stop=(c == 31))
                return e.matmul(bank[7][:, 0:257], lhsT=ones_b, rhs=s_[:, 257:514], start=(c == 0), stop=(c == 31))
            P.op("pe", mmn, reads=[("sq", c % 2), "ones_b"], writes=[("ps", 6), ("ps", 7)])
        P.op("act", lambda e: e.activation(out=rstd[:, 0:257], in_=bank[6][:, 0:257], func=AF.Sqrt, bias=1e-6, scale=1.0 / D), reads=[("ps", 6)], writes=["rstd"])
        P.op("act", lambda e: e.activation(out=rstd[:, 257:514], in_=bank[7][:, 0:257], func=AF.Sqrt, bias=1e-6, scale=1.0 / D), reads=[("ps", 7), "rstd"], writes=["rstd"])
        P.op("dve", lambda e: e.reciprocal(out=rstd, in_=rstd), reads=["rstd"], writes=["rstd"])
        for c in range(32):
            wcol = nrm_s[:, ncol * 32 + c:ncol * 32 + c + 1]
            if final_w is None:
                P.op("dve", lambda e, c=c, wcol=wcol: e.scalar_tensor_tensor(out=uT[:, c, :], in0=hT[:, c, :], scalar=wcol, in1=rstd, op0=ALU.mult, op1=ALU.mult),
                     reads=[("h", c), "rstd", "consts"], writes=[("u", c)])
            else:
                ot = otile[c % 2]
                P.op("dve", lambda e, c=c, wcol=wcol, ot=ot: e.scalar_tensor_tensor(out=ot, in0=hT[:, c, 1:513], scalar=wcol, in1=rstd[:, 1:513], op0=ALU.mult, op1=ALU.mult),
                     reads=[("h", c), "rstd", "consts"], writes=[("ot", c % 2)])
                P.dma("sp", outT[final_w, c], ot, reads=[("ot", c % 2)], writes=["outT"], sem_key=("dma", "ot%d" % (c % 2)))

    for w in range(n_win):
        P.barrier()
        A.reset(base_mark)
        oTw = A.alloc([32, NW], BF16)
        wo_t = [A.alloc([32, 128], BF16) for _ in range(3)]
        P.dma("sp", hT, xw[w].rearrange("p (c t) -> p c t", t=NW), writes=hkeys)
        P.dma("sp", oTw, ow[w].rearrange("p (c t) -> p c t", t=NW), writes=["oTw"])
        for dc in range(2):
            P.dma("pool", wo_t[dc % 3], wo[dc].rearrange("p (k n) -> p k n", n=128), writes=[("wo", dc % 3)])
        for dc in range(32):
            if dc + 2 < 32:
                P.dma("pool", wo_t[(dc + 2) % 3], wo[dc + 2].rearrange("p (k n) -> p k n", n=128), writes=[("wo", (dc + 2) % 3)])
            wt = wo_t[dc % 3]
            b0 = 2 * (dc % 2)

            def mmo(e, wt=wt, b0=b0):
                for half in range(2):
                    for k in range(32):
                        r = e.matmul(bank[b0 + half][:, 0:257], lhsT=wt[:, k, :], rhs=oTw[:, k, half * 257:(half + 1) * 257], start=(k == 0), stop=(k == 31))
                return r
            P.op("pe", mmo, reads=[("wo", dc % 3), "oTw"], writes=[("ps", b0), ("ps", b0 + 1)])
            for half in range(2):
                P.op("dve", lambda e, dc=dc, b0=b0, half=half: e.tensor_tensor(out=hT[:, dc, half * 257:(half + 1) * 257], in0=bank[b0 + half][:, 0:257],
                                                                              in1=hT[:, dc, half * 257:(half + 1) * 257], op=ALU.add),
                     reads=[("ps", b0 + half), ("h", dc)], writes=[("h", dc)])
        if stop_after == "A":
            break
        norm(0, w)
        P.barrier()
        A.reset(base_mark)
        wu_t = [A.alloc([32, 256], BF16) for _ in range(2)]
        wd_t = [[A.alloc([D], BF16) for _ in range(FG)] for _ in range(3)]
        act = [[A.alloc([NI], BF16) for _ in range(FG)] for _ in range(2)]
        gcv = [A.alloc([NI], F32) for _ in range(2)]
        ucv = [A.alloc([NI], F32) for _ in range(2)]
        ngrp = NFC // FG

        def load_wu(fc):
            P.dma("pool", wu_t[fc % 2], wu[fc].rearrange("p (k n) -> p k n", n=256), writes=[("wu", fc % 2)])

        def load_wd(g):
            for j in range(FG):
                P.dma("pool", wd_t[g % 3][j], wd[g * FG + j], writes=[("wd", g % 3, j)])

        def up(fc):
            g, j = fc // FG, fc % FG
            wt = wu_t[fc % 2]
            for part, (bb, dstl) in enumerate(((0, gcv), (2, ucv))):
                def mmu(e, wt=wt, part=part, bb=bb):
                    for hh in range(2):
                        for k in range(32):
                            r = e.matmul(bank[bb + hh][:, 0:258], lhsT=wt[:, k, part * 128:(part + 1) * 128], rhs=uT[:, k, hh * 256:hh * 256 + 258],
                                         start=(k == 0), stop=(k == 31))
                    return r
                P.op("pe", mmu, reads=[("wu", fc % 2)] + ukeys, writes=[("ps", bb), ("ps", bb + 1)])
                yield
                dst = dstl[fc % 2]
                dkey = ("cv", part, fc % 2)
                for hh in range(2):
                    o0 = hh * 256
                    cw = lambda jj, part=part, fc=fc: fcv_s[:, fc * 6 + part * 3 + jj:fc * 6 + part * 3 + jj + 1]
                    bk = bank[bb + hh]
                    P.op("dve", lambda e, dst=dst, o0=o0, bk=bk, cw=cw: e.scalar_tensor_tensor(out=dst[:, o0:o0 + 256], in0=bk[:, 0:256], scalar=cw(0), in1=zero_f,
                                                                                             op0=ALU.mult, op1=ALU.add),
                         reads=[("ps", bb + hh), "consts", "zero_f"], writes=[dkey + (hh,)])
                    for jj in (1, 2):
                        P.op("dve", lambda e, dst=dst, o0=o0, bk=bk, cw=cw, jj=jj: e.scalar_tensor_tensor(out=dst[:, o0:o0 + 256], in0=bk[:, jj:jj + 256], scalar=cw(jj),
                                                                                                       in1=dst[:, o0:o0 + 256], op0=ALU.mult, op1=ALU.add),
                             reads=[("ps", bb + hh), "consts", dkey + (hh,)], writes=[dkey + (hh,)])
            gk = [("cv", 0, fc % 2, 0), ("cv", 0, fc % 2, 1)]
            uk = [("cv", 1, fc % 2, 0), ("cv", 1, fc % 2, 1)]
            gd, ud = gcv[fc % 2], ucv[fc % 2]
            P.op("act", lambda e, gd=gd: e.activation(out=gd, in_=gd, func=AF.Silu), reads=gk, writes=gk)
            a_ = act[g % 2][j]
            P.op("dve", lambda e, gd=gd, ud=ud, a_=a_: e.tensor_tensor(out=a_, in0=gd, in1=ud, op=ALU.mult), reads=gk + uk, writes=[("act", g % 2, j)])

        def down(g):
            for dc in range(32):
                if dc % 8 == 0 and dc > 0:
                    yield
                bk = bank[4 + dc % 4]

                def mmd(e, dc=dc, bk=bk, g=g):
                    for j in range(FG):
                        r = e.matmul(bk[:, 0:512], lhsT=wd_t[g % 3][j][:, dc * 128:(dc + 1) * 128], rhs=act[g % 2][j], start=(j == 0), stop=(j == FG - 1))
                    return r
                P.op("pe", mmd, reads=[("wd", g % 3, j) for j in range(FG)] + [("act", g % 2, j) for j in range(FG)], writes=[("ps", 4 + dc % 4)])
                P.op("dve", lambda e, dc=dc, bk=bk: e.tensor_tensor(out=hT[:, dc, 1:513], in0=bk[:, 0:512], in1=hT[:, dc, 1:513], op=ALU.add),
                     reads=[("ps", 4 + dc % 4), ("h", dc)], writes=[("h", dc)])

        load_wu(0)
        load_wu(1)
        load_wd(0)
        def drain(gen):
            for _ in gen:
                pass

        for g in range(ngrp):
            dgen = down(g - 1) if g > 0 else iter(())
            if g + 1 < ngrp:
                load_wd(g + 1)
            for j in range(FG):
                fc = g * FG + j
                for _ in up(fc):
                    next(dgen, None)
                if fc + 2 < NFC:
                    load_wu(fc + 2)
            drain(dgen)
        drain(down(ngrp - 1))
        if stop_after == "B":
            break
        norm(1, w)
        P.barrier()
        A.reset(base_mark)
        wg_t = [A.alloc([32, 128], BF16) for _ in range(3)]
        wp_b = A.alloc([2, D], BF16)
        pT = A.alloc([2, NI], BF16)
        sig = [A.alloc([NI], F32) for _ in range(2)]
        otile = [A.alloc([NI], F32) for _ in range(2)]
        P.dma("pool", wp_b, wp.rearrange("p (k n) -> p k n", n=D), writes=["wp"])
        P.dma("pool", pT, pw[w].rearrange("p (k t) -> p k t", t=NI), writes=["pT"])
        for dc in range(2):
            P.dma("pool", wg_t[dc % 3], wgt[dc].rearrange("p (k n) -> p k n", n=128), writes=[("wg", dc % 3)])
        for dc in range(32):
            if dc + 2 < 32:
                P.dma("pool", wg_t[(dc + 2) % 3], wgt[dc + 2].rearrange("p (k n) -> p k n", n=128), writes=[("wg", (dc + 2) % 3)])
            wt = wg_t[dc % 3]
            bg, bp = bank[dc % 2], bank[2 + dc % 2]

            def mmg(e, wt=wt, bg=bg):
                for k in range(32):
                    r = e.matmul(bg[:, 0:512], lhsT=wt[:, k, :], rhs=uT[:, k, 1:513], start=(k == 0), stop=(k == 31))
                return r
            P.op("pe", mmg, reads=[("wg", dc % 3)] + ukeys, writes=[("ps", dc % 2)])

            def mmp(e, dc=dc, bp=bp):
                for k in range(2):
                    r = e.matmul(bp[:, 0:512], lhsT=wp_b[:, k, dc * 128:(dc + 1) * 128], rhs=pT[:, k, :], start=(k == 0), stop=(k == 1))
                return r
            P.op("pe", mmp, reads=["wp", "pT"], writes=[("ps", 2 + dc % 2)])
            sg = sig[dc % 2]
            P.op("act", lambda e, sg=sg, bg=bg: e.activation(out=sg, in_=bg[:, 0:512], func=AF.Sigmoid), reads=[("ps", dc % 2)], writes=[("sig", dc % 2)])
            P.op("dve", lambda e, sg=sg, bp=bp: e.tensor_tensor(out=sg, in0=sg, in1=bp[:, 0:512], op=ALU.mult), reads=[("ps", 2 + dc % 2), ("sig", dc % 2)], writes=[("sig", dc % 2)])
            P.op("dve", lambda e, sg=sg, dc=dc: e.tensor_tensor(out=hT[:, dc, 1:513], in0=sg, in1=hT[:, dc, 1:513], op=ALU.add), reads=[("sig", dc % 2), ("h", dc)], writes=[("h", dc)])
        norm(2, w, final_w=w)

    P.op("sp", lambda e: e.nop(), reads=["outT"] + hkeys + ukeys)
    P.emit()
    for g in reversed(psg):
        g.__exit__(None, None, None)
    A.close()
    return nc


def l2_weights(inp):
    f = np.float32
    tile_cols = lambda w: np.ascontiguousarray(w.reshape(32, 128, w.shape[1] // 128, 128).transpose(2, 1, 0, 3)).reshape(w.shape[1] // 128, 128, 32 * 128)
    w_up = inp["w_up"][0]
    gt = w_up[:, :11008].reshape(32, 128, NFC, 128)
    ut = w_up[:, 11008:].reshape(32, 128, NFC, 128)
    wu = np.ascontiguousarray(np.concatenate([gt, ut], axis=3).transpose(2, 1, 0, 3)).reshape(NFC, 128, 32 * 256)
    fc = inp["ffn_conv"][0]
    fcv = np.ascontiguousarray(np.stack([fc[:, :11008].reshape(3, NFC, 128), fc[:, 11008:].reshape(3, NFC, 128)], axis=0).transpose(3, 2, 0, 1)).reshape(128, NFC * 6)
    nrm = np.ascontiguousarray(np.concatenate([inp["ffn_norm"][0].reshape(32, 128).T, inp["ple_norm"][0].reshape(32, 128).T,
                                               inp["final_norm"].reshape(32, 128).T], axis=1))
    return {
        "wo": tile_cols(inp["w_out"][0]), "wu": wu, "wd": np.ascontiguousarray(inp["w_down"][0].reshape(NFC, 128, D)),
        "wgt": tile_cols(inp["w_ple_gate"][0]), "wp": _ptile(inp["w_ple_proj"][0]), "nrm": nrm.astype(f), "fcv": fcv.astype(f),
    }


def l2_inputs(inp, oT, core, wts):
    b, tq = core // 4, core % 4
    T0 = 1024 * tq
    x = inp["x"][b]
    p = inp["p"][0, b]
    xw = np.zeros((2, 128, 32, NW), np.float32)
    ow = np.zeros((2, 128, 32, NW), oT.dtype)
    pw = np.zeros((2, 128, 2, NI), np.float32)
    for w in range(2):
        lo = T0 + 512 * w - 1
        a, e = max(lo, 0), min(lo + NW, S)
        xw[w, :, :, a - lo:e - lo] = x[a:e].T.reshape(32, 128, e - a).transpose(1, 0, 2)
        ow[w, :, :, a - lo:e - lo] = oT[b][:, a:e].reshape(32, 128, e - a).transpose(1, 0, 2)
        pw[w] = p[T0 + 512 * w:T0 + 512 * w + NI].T.reshape(2, 128, NI).transpose(1, 0, 2)
    m = dict(wts)
    m.update({"xw": xw.reshape(2, 128, 32 * NW), "ow": ow.reshape(2, 128, 32 * NW), "pw": pw.reshape(2, 128, 2 * NI)})
    return m


def run_l2(inputs, oT):
    nc = build_l2()
    wts = l2_weights(inputs)
    in_maps = [l2_inputs(inputs, oT, c, wts) for c in range(8)]
    res = run_bass_kernel_spmd(nc, in_maps, core_ids=list(range(8)))
    out = np.zeros((2, S, D), np.float32)
    for c in range(8):
        b, tq = c // 4, c % 4
        o = np.asarray(res.results[c]["outT"])
        for w in range(2):
            t0 = 1024 * tq + 512 * w
            out[b, t0:t0 + NI] = o[w].reshape(D, NI).T
    return out


def kernel(**inputs):
    inputs = {k: np.asarray(v) for k, v in inputs.items()}
    oT = run_l1(inputs)
    return run_l2(inputs, oT)
```

```python
import numpy as np
import concourse.bass as bass
import concourse.mybir as mybir
from concourse.bass_utils import run_bass_kernel_spmd

F32 = mybir.dt.float32
BF16 = mybir.dt.bfloat16
U8 = mybir.dt.uint8
F32R = mybir.dt.float32r
AF = mybir.ActivationFunctionType
ALU = mybir.AluOpType

ENG_NAMES = ("pe", "dve", "act", "pool", "sp")
D = 4096
S = 4096
NEG = -60000.0


class _Op:
    __slots__ = ("eng", "fn", "deps", "is_dma", "sem_key", "sig", "id")


class Prog:
    def __init__(self, nc, self_sync=("dve", "act", "pool")):
        self.nc = nc
        self.ops = []
        self.by_eng = {e: [] for e in ENG_NAMES}
        self.last_write = {}
        self.reads_since = {}
        self.dma_keys = {}
        self.self_sync = set(self_sync)
        self.global_dep = None

    def op(self, eng, fn, reads=(), writes=(), dma=False, sem_key=None):
        o = _Op()
        o.eng, o.fn, o.is_dma, o.sem_key, o.sig = eng, fn, dma, sem_key, False
        o.id = len(self.ops)
        psr = [r for r in reads if isinstance(r, tuple) and r[0] == "ps"]
        if psr:
            writes = list(writes) + psr
            reads = [r for r in reads if r not in psr]
        deps = set()
        if self.global_dep is not None:
            deps.add(self.global_dep)
        for r in reads:
            lw = self.last_write.get(r)
            if lw is not None:
                deps.add(lw)
        for w in writes:
            lw = self.last_write.get(w)
            if lw is not None:
                deps.add(lw)
            deps.update(self.reads_since.get(w, ()))
        o.deps = deps
        for r in reads:
            self.reads_since.setdefault(r, []).append(o.id)
        for w in writes:
            self.last_write[w] = o.id
            self.reads_since[w] = []
        if dma:
            if sem_key is None:
                o.sem_key = ("dma", tuple(writes)[0])
            self.dma_keys.setdefault(o.sem_key, 0)
        self.ops.append(o)
        self.by_eng[eng].append(o)
        return o.id

    def dma(self, eng, out, in_, reads=(), writes=(), sem_key=None):
        return self.op(eng, lambda e: e.dma_start(out=out, in_=in_), reads, writes, dma=True, sem_key=sem_key)

    def barrier(self):
        keys = set(self.last_write) | set(self.reads_since)
        bid = self.op("sp", lambda e: e.nop(), reads=(), writes=tuple(keys))
        self.global_dep = bid

    def emit(self):
        nc = self.nc
        ops = self.ops
        dma_issued = {}
        need = {}
        for o in ops:
            waits = {}
            for d in o.deps:
                y = ops[d]
                if y.is_dma:
                    k = ("D", y.sem_key)
                    waits[k] = max(waits.get(k, 0), dma_issued.get(y.sem_key, 0))
                else:
                    if y.eng == o.eng and y.eng not in self.self_sync:
                        continue
                    y.sig = True
                    k = ("E", y.eng)
                    waits[k] = max(waits.get(k, -1), d)
            need[o.id] = waits
            if o.is_dma:
                dma_issued[o.sem_key] = dma_issued.get(o.sem_key, 0) + 1
        sigidx = {}
        for e in ENG_NAMES:
            c = 0
            for o in self.by_eng[e]:
                if o.sig and not o.is_dma:
                    c += 1
                    sigidx[o.id] = c
        ctx = []
        esem = {}
        for e in ENG_NAMES:
            g = nc.semaphore("prog_" + e)
            esem[e] = g.__enter__()
            ctx.append(g)
        dsem = {}
        for i, k in enumerate(self.dma_keys.keys()):
            g = nc.semaphore("dsem_%d" % i)
            dsem[k] = g.__enter__()
            ctx.append(g)
        engobj = {"pe": "tensor", "dve": "vector", "act": "scalar", "pool": "gpsimd", "sp": "sync"}
        blk = nc.Block()
        block = blk.__enter__()
        for e in ENG_NAMES:
            lst = self.by_eng[e]
            if not lst:
                continue

            def body(eng, lst=lst, e=e):
                waited = {}
                for o in lst:
                    for k, v in need[o.id].items():
                        if k[0] == "D":
                            sem, val = dsem[k[1]], 16 * v
                        else:
                            sem, val = esem[k[1]], sigidx[v]
                        if waited.get(k, 0) >= val:
                            continue
                        waited[k] = val
                        eng.wait_ge(sem, val)
                    ins = o.fn(eng)
                    if o.is_dma:
                        ins.then_inc(dsem[o.sem_key], 16)
                    elif o.sig:
                        ins.then_inc(esem[e], 1)

            getattr(block, engobj[e])(body)
        blk.__exit__(None, None, None)
        for g in reversed(ctx):
            g.__exit__(None, None, None)


class Arena:
    def __init__(self, nc, nbytes):
        self.g = nc.sbuf_tensor("arena", [128, nbytes], U8)
        self.t = self.g.__enter__()
        self.n = nbytes
        self.off = 0

    def mark(self):
        return self.off

    def reset(self, m):
        self.off = m

    def alloc(self, shape, dt):
        esz = 4 if dt == F32 else 2
        n = 1
        for s in shape:
            n *= s
        nb = (n * esz + 63) // 64 * 64
        assert self.off + nb <= self.n, ("arena overflow", self.off, nb, self.n)
        v = self.t[:, self.off:self.off + n * esz].bitcast(dt)
        self.off += nb
        if len(shape) == 2:
            v = v.rearrange("p (a b) -> p a b", a=shape[0])
        elif len(shape) == 3:
            v = v.rearrange("p (a b c) -> p a b c", a=shape[0], b=shape[1])
        return v

    def close(self):
        self.g.__exit__(None, None, None)


def _consts_np():
    j = np.arange(128)[:, None]
    i = np.arange(128)[None, :]
    c = {}
    c["ident"] = (j == i).astype(np.float32)
    c["triL"] = (j <= i).astype(np.float32)
    c["triU"] = (j >= i).astype(np.float32)
    c["m_incl_f"] = np.where(i >= j, 0.0, NEG).astype(np.float32)
    c["m_incl_b"] = np.where(i <= j, 0.0, NEG).astype(np.float32)
    c["m_str_f"] = np.where(i > j, 0.0, NEG).astype(np.float32)
    c["m_str_b"] = np.where(i < j, 0.0, NEG).astype(np.float32)
    c["m_pos_f"] = np.where(j > i, 0.0, -NEG).astype(np.float32)
    c["m_pos_b"] = np.where(j < i, 0.0, -NEG).astype(np.float32)
    return c


CONST_NAMES = ["ident", "triL", "triU", "m_incl_f", "m_incl_b", "m_str_f", "m_str_b", "m_pos_f", "m_pos_b"]
import os as _os
PATTERNS = tuple(int(v) for v in _os.environ.get('PATS', '1,4,16').split(','))


def _alibi_tables(heads):
    kk = np.arange(128)[:, None]
    qq = np.arange(256)[None, :]
    dist = np.abs(kk - qq + 64)
    out = np.zeros((len(heads), 9, 128, 256), np.float32)
    for hi, h in enumerate(heads):
        slope = 2.0 ** (-8.0 * (h + 1.0) / 16.0)
        for pi, d in enumerate(PATTERNS):
            base = np.where(dist <= 64, -slope * d * dist.astype(np.float64), NEG)
            first = base.copy()
            first[:64, :] = NEG
            last = base.copy()
            last[64:, :] = NEG
            out[hi, pi * 3 + 0] = base
            out[hi, pi * 3 + 1] = first
            out[hi, pi * 3 + 2] = last
    return out


def build_l1(n_gdn=4, n_att=4, dbg=False):
    nc = bass.Bass("TRN2", target_bir_lowering=False)
    dram = lambda name, shape, dt, kind="ExternalInput": nc.dram_tensor(name, shape, dt, kind=kind).ap()
    xT = dram("xT", [16, 128, 32 * 256], F32)
    anw = dram("anw", [128, 32], F32)
    wg = dram("wg", [4, 128, 32 * 512], F32)
    wab = dram("wab", [128, 32 * 16], F32)
    wa = dram("wa", [4, 128, 32 * 384], F32)
    gconv = dram("gconv", [128, 36], F32)
    galog = dram("galog", [128, 8], F32)
    gdtb = dram("gdtb", [128, 8], F32)
    gnw = dram("gnw", [128, 1], F32)
    cst = dram("cst", [len(CONST_NAMES), 128, 128], F32)
    alibi = dram("alibi", [4, 9, 128, 256], F32)
    oT = dram("oT", [8, 128, S], BF16, kind="ExternalOutput")
    uT_d = dram("uT_d", [16, 128, 32 * 256], BF16, kind="Internal")
    dbg_o = dram("dbg_o", [128, 3 * (S + 2048)], BF16, kind="ExternalOutput") if dbg else None

    P = Prog(nc)
    A = Arena(nc, 204 * 1024)
    psg = [nc.psum_tensor("bank%d" % i, [128, 512], F32) for i in range(8)]
    bank = [g.__enter__() for g in psg]

    cf = A.alloc([len(CONST_NAMES), 128], F32)
    cb_ident = A.alloc([128], BF16)
    ones_b = A.alloc([128], BF16)
    ones_f = A.alloc([128], F32)
    anw_s = A.alloc([32], F32)
    gconv_s = A.alloc([36], F32)
    galog_s = A.alloc([8], F32)
    gdtb_s = A.alloc([8], F32)
    gnw_s = A.alloc([1], F32)
    wab_b = A.alloc([32, 16], BF16)
    g_col = A.alloc([32, 8], F32)
    lnb = A.alloc([32, 8], F32)
    beta = A.alloc([32, 8], F32)
    Gc = A.alloc([32, 8], F32)
    negG = A.alloc([32, 8], F32)
    Gb = A.alloc([32, 8], F32)
    cbe = A.alloc([32, 8], F32)
    cdl = A.alloc([32, 8], F32)
    ekd = A.alloc([32, 8], F32)
    gtmp = A.alloc([32, 8], F32)
    gtmp2 = A.alloc([32, 8], F32)
    C = {n: cf[:, i, :] for i, n in enumerate(CONST_NAMES)}

    P.dma("sp", cf, cst.rearrange("n p f -> p n f"), writes=["cf"])
    P.dma("sp", anw_s, anw, writes=["small"])
    P.dma("sp", gconv_s, gconv, writes=["small"], sem_key=("dma", "small"))
    P.dma("sp", galog_s, galog, writes=["small"], sem_key=("dma", "small"))
    P.dma("sp", gdtb_s, gdtb, writes=["small"], sem_key=("dma", "small"))
    P.dma("sp", gnw_s, gnw, writes=["small"], sem_key=("dma", "small"))
    P.dma("pool", wab_b, wab.rearrange("p (c n) -> p c n", n=16), writes=["wab"])
    P.op("dve", lambda e: e.tensor_copy(out=cb_ident, in_=C["ident"]), reads=["cf"], writes=["cbi"])
    P.op("dve", lambda e: e.memset(ones_b, 1.0), writes=["ones_b"])
    P.op("dve", lambda e: e.memset(ones_f, 1.0), writes=["ones_f"])

    wbuf = [A.alloc([32, 512], BF16)] * 2
    base_mark = A.mark()

    heads = [("g", h) for h in range(n_gdn)] + [("a", h) for h in range(n_att)]

    def load_w(hi):
        kind, h = heads[hi]
        if kind == "g":
            P.dma("pool", wbuf[hi % 2], wg[h].rearrange("p (c n) -> p c n", n=512), writes=[("w", 0)])
        else:
            P.dma("pool", wbuf[hi % 2][:, :, 0:384], wa[h].rearrange("p (c n) -> p c n", n=384), writes=[("w", 0)])

    if heads:
        load_w(0)

    utile = [A.alloc([32, 256], BF16) for _ in range(2)]
    xs = [A.alloc([32, 256], F32) for _ in range(2)]
    sq = [A.alloc([256], BF16) for _ in range(4)]
    rst = A.alloc([256], F32)
    for tt in range(16):
        xb = xs[tt % 2]
        ub = utile[tt % 2]
        P.dma("sp", xb, xT[tt].rearrange("p (c t) -> p c t", t=256), writes=[("xs", tt % 2)])
        for c in range(32):
            s_ = sq[c % 4]
            P.op("act", lambda e, s_=s_, xb=xb, c=c: e.activation(out=s_, in_=xb[:, c, :], func=AF.Square),
                 reads=[("xs", tt % 2)], writes=[("sq", c % 4)])
            P.op("pe", lambda e, s_=s_, c=c: e.matmul(bank[0][:, 0:256], lhsT=ones_b, rhs=s_, start=(c == 0), stop=(c == 31)),
                 reads=[("sq", c % 4), "ones_b"], writes=[("ps", 0)])
        P.op("act", lambda e: e.activation(out=rst, in_=bank[0][:, 0:256], func=AF.Sqrt, bias=1e-6, scale=1.0 / D),
             reads=[("ps", 0)], writes=["rst"])
        P.op("dve", lambda e: e.reciprocal(out=rst, in_=rst), reads=["rst"], writes=["rst"])
        for c in range(32):
            P.op("dve", lambda e, xb=xb, ub=ub, c=c: e.scalar_tensor_tensor(out=ub[:, c, :], in0=xb[:, c, :], scalar=anw_s[:, c:c + 1], in1=rst,
                                                                         op0=ALU.mult, op1=ALU.mult),
                 reads=[("xs", tt % 2), "rst", "small"], writes=[("ut", tt % 2, c)])
        P.dma("pool", uT_d[tt].rearrange("p (c t) -> p c t", t=256), ub, reads=[("ut", tt % 2, c) for c in range(32)], writes=["uT_d"],
              sem_key=("dma", "uT_d"))
    P.barrier()
    A.reset(base_mark)

    for hi, (kind, h) in enumerate(heads):
        wb = wbuf[0]
        wkey = ("w", 0)
        ncc = 4 if kind == "g" else 3
        if kind == "g":
            zs = A.alloc([S], F32)
            qkvn = A.alloc([3, S], F32)
            m_q = A.mark()
            pj = A.alloc([3, S + 2], F32)
            m_pj = A.mark()
            P.op("pool", lambda e, pj=pj: e.memset(pj[:, :, 0:1], 0.0), writes=["pj"])
            P.op("pool", lambda e, pj=pj: e.memset(pj[:, :, S + 1:S + 2], 0.0), writes=["pj"])
        else:
            qkv = A.alloc([3, S + 2048], BF16)
            P.op("pool", lambda e, qkv=qkv: e.memset(qkv[:, :, 0:1024], 0.0), writes=["qkv"])
            P.op("pool", lambda e, qkv=qkv: e.memset(qkv[:, :, 1024 + S:2048 + S], 0.0), writes=["qkv"])
        utile = [A.alloc([32, 256], BF16) for _ in range(2)]
        do_ab = (kind == "g" and h == 0)
        for tt in range(16):
            ub = utile[tt % 2]
            P.dma("sp", ub, uT_d[tt].rearrange("p (c t) -> p c t", t=256), reads=["uT_d"], writes=[("utl", tt % 2)])
            for cc in range(ncc):
                bk = (tt * ncc + cc) % 4

                def mm(e, ub=ub, cc=cc, bk=bk):
                    for k in range(32):
                        r = e.matmul(bank[bk][:, 0:256], lhsT=wb[:, k, cc * 128:(cc + 1) * 128], rhs=ub[:, k, :], start=(k == 0), stop=(k == 31))
                    return r
                P.op("pe", mm, reads=[wkey, ("utl", tt % 2)], writes=[("ps", bk)])
                sl = slice(tt * 256, (tt + 1) * 256)
                if kind == "g":
                    if cc < 3:
                        dst = pj[:, cc, 1 + tt * 256:1 + (tt + 1) * 256]
                        eng = "act" if cc % 2 == 0 else "dve"
                        if eng == "act":
                            P.op("act", lambda e, dst=dst, bk=bk: e.activation(out=dst, in_=bank[bk][:, 0:256], func=AF.Copy),
                                 reads=[("ps", bk)], writes=[("pj", cc, tt)])
                        else:
                            P.op("dve", lambda e, dst=dst, bk=bk: e.tensor_copy(out=dst, in_=bank[bk][:, 0:256]),
                                 reads=[("ps", bk)], writes=[("pj", cc, tt)])
                    else:
                        P.op("act", lambda e, bk=bk, sl=sl: e.activation(out=zs[:, sl], in_=bank[bk][:, 0:256], func=AF.Silu),
                             reads=[("ps", bk)], writes=[("zs", tt)])
                else:
                    dst = qkv[:, cc, 1024 + tt * 256:1024 + (tt + 1) * 256]
                    if cc == 0:
                        P.op("act", lambda e, dst=dst, bk=bk: e.activation(out=dst, in_=bank[bk][:, 0:256], func=AF.Copy, scale=128.0 ** -0.5),
                             reads=[("ps", bk)], writes=[("qkv", cc, tt)])
                    elif cc == 1:
                        P.op("dve", lambda e, dst=dst, bk=bk: e.tensor_copy(out=dst, in_=bank[bk][:, 0:256]),
                             reads=[("ps", bk)], writes=[("qkv", cc, tt)])
                    else:
                        P.op("act", lambda e, dst=dst, bk=bk: e.activation(out=dst, in_=bank[bk][:, 0:256], func=AF.Copy),
                             reads=[("ps", bk)], writes=[("qkv", cc, tt)])
            if do_ab:
                for half in range(2):
                    ch = tt * 2 + half

                    def mmab(e, ub=ub, half=half, ch=ch):
                        for k in range(32):
                            r = e.matmul(bank[4][:, ch * 16:(ch + 1) * 16], lhsT=ub[:, k, half * 128:(half + 1) * 128], rhs=wab_b[:, k, :],
                                         start=(k == 0), stop=(k == 31))
                        return r
                    P.op("pe", mmab, reads=["wab", ("utl", tt % 2)], writes=[("ps", 4)])
        if hi + 1 < len(heads):
            load_w(hi + 1)
        if do_ab:
            _gates(P, bank, C, ones_f, galog_s, gdtb_s, g_col, lnb, beta, Gc, negG, Gb, cbe, cdl, ekd, gtmp, gtmp2)
        if dbg and kind == "a" and h == 0:
            P.dma("sp", dbg_o.rearrange("p (a b) -> p a b", a=3), qkv, reads=[("qkv", t, tt) for t in range(3) for tt in range(16)] + ["qkv"], writes=["dbg_o"])
        if kind == "g" and _os.environ.get("SKIPG"):
            pass
        elif kind == "g":
            P.barrier()
            A.reset(m_pj)
            _gdn_conv(P, A, bank, ones_b, C["ident"], h, pj, qkvn, gconv_s)
            P.barrier()
            A.reset(m_q)
            _gdn_chunks(P, A, bank, C, ones_b, h, hi, qkvn, zs, gnw_s, g_col, lnb, beta, Gc, negG, Gb, cbe, cdl, ekd, oT)
        else:
            _att_head(P, A, bank, cb_ident, ones_b, h, hi, qkv, alibi, oT)
        P.barrier()
        A.reset(base_mark)

    P.op("sp", lambda e: e.nop(), reads=["oT", "dbg_o"])
    P.emit()
    for g in reversed(psg):
        g.__exit__(None, None, None)
    A.close()
    return nc


def _gates(P, bank, C, ones_f, galog_s, gdtb_s, g_col, lnb, beta, Gc, negG, Gb, cbe, cdl, ekd, t1, t2):
    ab = bank[4][:, 0:512].rearrange("p (c k) -> p c k", k=16)
    rd = [("ps", 4), "small"]
    bc = lambda a: a.unsqueeze(1).to_broadcast([128, 32, 8])
    P.op("dve", lambda e: e.tensor_tensor(out=t1, in0=ab[:, :, 0:8], in1=bc(gdtb_s), op=ALU.add), reads=rd, writes=["t1"])
    P.op("act", lambda e: e.activation(out=t1, in_=t1, func=AF.Exp), reads=["t1"], writes=["t1"])
    P.op("act", lambda e: e.activation(out=t1, in_=t1, func=AF.Ln, bias=1.0), reads=["t1"], writes=["t1"])
    P.op("act", lambda e: e.activation(out=t2[:, 0, :], in_=galog_s, func=AF.Exp), reads=["small"], writes=["t2"])
    P.op("dve", lambda e: e.scalar_tensor_tensor(out=g_col, in0=t1, scalar=-1.0, in1=t2[:, 0, :].unsqueeze(1).to_broadcast([128, 32, 8]),
                                                 op0=ALU.mult, op1=ALU.mult), reads=["t1", "t2"], writes=["gate"])
    P.op("act", lambda e: e.activation(out=t2, in_=ab[:, :, 8:16], func=AF.Exp, scale=-1.0), reads=rd + ["gate"], writes=["t2"])
    P.op("act", lambda e: e.activation(out=t2, in_=t2, func=AF.Ln, bias=1.0), reads=["t2"], writes=["t2"])
    P.op("dve", lambda e: e.tensor_scalar(out=lnb, in0=t2, scalar1=-1.0, scalar2=None, op0=ALU.mult), reads=["t2"], writes=["gate2"])
    P.op("act", lambda e: e.activation(out=beta, in_=t2, func=AF.Exp, scale=-1.0), reads=["t2"], writes=["gate3"])
    P.op("pe", lambda e: e.matmul(bank[5][:, 0:128].rearrange("p (c k) -> p c k", k=4), lhsT=C["triL"], rhs=g_col[:, :, 0:4], start=True, stop=True),
         reads=["gate", "cf"], writes=[("ps", 5)])
    P.op("pe", lambda e: e.matmul(bank[5][:, 128:256].rearrange("p (c k) -> p c k", k=4), lhsT=C["triU"], rhs=g_col[:, :, 4:8], start=True, stop=True),
         reads=["gate", "cf"], writes=[("ps", 5)])
    P.op("pe", lambda e: e.matmul(bank[5][:, 256:512].rearrange("p (c k) -> p c k", k=8), lhsT=ones_f, rhs=g_col, start=True, stop=True),
         reads=["gate", "ones_f"], writes=[("ps", 5)])
    P.op("dve", lambda e: e.tensor_copy(out=Gc[:, :, 0:4], in_=bank[5][:, 0:128].rearrange("p (c k) -> p c k", k=4)), reads=[("ps", 5)], writes=["Gc"])
    P.op("dve", lambda e: e.tensor_copy(out=Gc[:, :, 4:8], in_=bank[5][:, 128:256].rearrange("p (c k) -> p c k", k=4)), reads=[("ps", 5)], writes=["Gc"])
    P.op("dve", lambda e: e.tensor_scalar(out=negG, in0=Gc, scalar1=-1.0, scalar2=None, op0=ALU.mult), reads=["Gc"], writes=["negG"])
    P.op("dve", lambda e: e.tensor_tensor(out=Gb, in0=Gc, in1=lnb, op=ALU.add), reads=["Gc", "gate2"], writes=["Gb"])
    P.op("act", lambda e: e.activation(out=cbe, in_=Gb, func=AF.Exp), reads=["Gb"], writes=["cbe"])
    P.op("act", lambda e: e.activation(out=cdl, in_=bank[5][:, 256:512].rearrange("p (c k) -> p c k", k=8), func=AF.Exp), reads=[("ps", 5)], writes=["cdl"])
    P.op("dve", lambda e: e.tensor_tensor(out=t1, in0=bank[5][:, 256:512].rearrange("p (c k) -> p c k", k=8), in1=Gc, op=ALU.subtract),
         reads=[("ps", 5), "Gc", "t1"], writes=["t1"])
    P.op("act", lambda e: e.activation(out=ekd, in_=t1, func=AF.Exp), reads=["t1"], writes=["ekd"])


def _gdn_conv(P, A, bank, ones_b, ident_f, h, pj, qkvn, gconv_s):
    acc = [A.alloc([512], F32) for _ in range(2)]
    sqb = [A.alloc([512], BF16) for _ in range(2)]
    rs = [A.alloc([512], F32) for _ in range(2)]
    it = 0
    for t in range(3):
        for tl in range(8):
            a_ = acc[it % 2]
            s0 = tl * 512
            cw = lambda j, t=t: gconv_s[:, h * 9 + t * 3 + j:h * 9 + t * 3 + j + 1]
            ak = ("acc", it % 2)
            P.op("dve", lambda e, a_=a_, s0=s0, cw=cw, t=t: e.tensor_scalar(out=a_, in0=pj[:, t, s0:s0 + 512], scalar1=cw(0), scalar2=None, op0=ALU.mult),
                 reads=["small"], writes=[ak])
            P.op("dve", lambda e, a_=a_, s0=s0, cw=cw, t=t: e.scalar_tensor_tensor(out=a_, in0=pj[:, t, s0 + 1:s0 + 513], scalar=cw(1), in1=a_, op0=ALU.mult, op1=ALU.add),
                 reads=[ak], writes=[ak])
            P.op("dve", lambda e, a_=a_, s0=s0, cw=cw, t=t: e.scalar_tensor_tensor(out=a_, in0=pj[:, t, s0 + 2:s0 + 514], scalar=cw(2), in1=a_, op0=ALU.mult, op1=ALU.add),
                 reads=[ak], writes=[ak])
            if t == 2:
                P.op("act", lambda e, a_=a_, s0=s0: e.activation(out=qkvn[:, 2, s0:s0 + 512], in_=a_, func=AF.Silu), reads=[ak], writes=[("qkvn", 2, tl)])
            else:
                P.op("act", lambda e, a_=a_: e.activation(out=a_, in_=a_, func=AF.Silu), reads=[ak], writes=[ak])
                sb_ = sqb[it % 2]
                r_ = rs[it % 2]
                P.op("act", lambda e, a_=a_, sb_=sb_: e.activation(out=sb_, in_=a_, func=AF.Square), reads=[ak], writes=[("sqb", it % 2)])
                def mmss(e, sb_=sb_):
                    for c4 in range(4):
                        r = e.matmul(bank[7][:, c4:c4 + 1], lhsT=sb_[:, c4 * 128:(c4 + 1) * 128], rhs=ones_b[:, 0:1], start=True, stop=True)
                    return r
                P.op("pe", mmss, reads=[("sqb", it % 2), "ones_b"], writes=[("ps", 7)])
                P.op("act", lambda e, r_=r_: e.activation(out=r_[:, 0:4], in_=bank[7][:, 0:4], func=AF.Sqrt, bias=1e-6), reads=[("ps", 7)], writes=[("rs", it % 2)])
                P.op("dve", lambda e, r_=r_: e.reciprocal(out=r_[:, 0:4], in_=r_[:, 0:4]), reads=[("rs", it % 2)], writes=[("rs", it % 2)])

                def mmbc(e, r_=r_):
                    for c4 in range(4):
                        r = e.matmul(bank[6][:, c4 * 128:(c4 + 1) * 128], lhsT=r_[:, c4:c4 + 1].to_broadcast([128, 128]), rhs=ident_f, start=True, stop=True)
                    return r
                P.op("pe", mmbc, reads=[("rs", it % 2), "cf"], writes=[("ps", 6)])
                sc = 128.0 ** -0.5 if t == 0 else 1.0
                P.op("dve", lambda e, a_=a_, s0=s0, t=t, sc=sc: e.scalar_tensor_tensor(out=qkvn[:, t, s0:s0 + 512], in0=a_, scalar=sc, in1=bank[6][:, 0:512], op0=ALU.mult, op1=ALU.mult),
                     reads=[ak, ("ps", 6)], writes=[("qkvn", t, tl)])
            it += 1


def _gdn_chunks(P, A, bank, C, ones_b, h, hi, qkvn, zs, gnw_s, g_col, lnb, beta, Gc, negG, Gb, cbe, cdl, ekd, oT):
    gate_keys = ["gate", "gate2", "gate3", "Gc", "negG", "Gb", "cbe", "cdl", "ekd"]
    oacc = A.alloc([S], F32)
    obf = A.alloc([S], BF16)
    acc = [A.alloc([512], F32) for _ in range(2)]
    sqb = [A.alloc([512], BF16) for _ in range(2)]
    rs = [A.alloc([512], F32) for _ in range(2)]
    P.op("pool", lambda e: e.memset(oacc, 0.0), writes=["oacc"])
    NSL = 2
    sl = {}
    names = ["kbg", "kd", "vb", "ktm", "vtm", "E1", "E2", "E3", "Eq", "M0", "M1", "MT0", "MT1", "Pf", "IT", "qd", "nWT", "vnew"]
    NEUB = _os.environ.get("NEU_BF16", "0") == "1"
    ACTS = _os.environ.get("ACT_SCALE", "1") == "1"
    for d in range(2):
        for s_ in range(NSL):
            sl[(d, s_)] = {n: A.alloc([128], F32) for n in names}
            for n in ("Mb0", "Mb1", "MTb0", "MTb1", "Pb0", "Pb1"):
                sl[(d, s_)][n] = A.alloc([128], BF16)
    Sf = [A.alloc([128], F32) for _ in range(2)]
    for d in range(2):
        P.op("pool", lambda e, d=d: e.memset(Sf[d], 0.0), writes=[("Sf", d)])
    ident = C["ident"]
    rr = (lambda a: a.bitcast(F32R)) if _os.environ.get("F32R", "0") == "1" else (lambda a: a)

    def chunk(d, step):
        c = step if d == 0 else 31 - step
        s_ = step % NSL
        B = sl[(d, s_)]
        K = lambda n: (n, d, s_)
        gi = d * 4 + h
        cs = slice(c * 128, (c + 1) * 128)
        qT = qkvn[:, 0, cs]
        kT = qkvn[:, 1, cs]
        vT = qkvn[:, 2, cs]
        qk_r = [("qkvn", t, c // 4) for t in range(3)]
        bA = bank[0 + d]
        bB = bank[2 + 2 * d + step % 2]
        bC = bank[6 + d]
        kA, kB, kC = ("ps", 0 + d), ("ps", 2 + 2 * d + step % 2), ("ps", 6 + d)
        tri = C["triL"] if d == 0 else C["triU"]
        m_incl = C["m_incl_f"] if d == 0 else C["m_incl_b"]
        m_str = C["m_str_f"] if d == 0 else C["m_str_b"]
        m_pos = C["m_pos_f"] if d == 0 else C["m_pos_b"]
        col = lambda arr: arr[:, c, gi:gi + 1]
        gbc = col(g_col).to_broadcast([128, 128])
        lbc = col(lnb).to_broadcast([128, 128])
        P.op("pe", lambda e: e.transpose(bC[:, 0:128], kT, ident), reads=qk_r + ["cf"], writes=[kC])
        P.op("pe", lambda e: e.transpose(bC[:, 128:256], vT, ident), reads=qk_r + ["cf"], writes=[kC])
        if ACTS:
            P.op("act", lambda e: e.activation(out=B["kbg"], in_=bC[:, 0:128], func=AF.Copy, scale=col(cbe)), reads=[kC] + gate_keys, writes=[K("kbg")])
            P.op("act", lambda e: e.activation(out=B["kd"], in_=bC[:, 0:128], func=AF.Copy, scale=col(ekd)), reads=[kC] + gate_keys, writes=[K("kd")])
            P.op("act", lambda e: e.activation(out=B["vb"], in_=bC[:, 128:256], func=AF.Copy, scale=col(beta)), reads=[kC] + gate_keys, writes=[K("vb")])
        else:
            P.op("dve", lambda e: e.tensor_copy(out=B["ktm"], in_=bC[:, 0:128]), reads=[kC], writes=[K("ktm")])
            P.op("act", lambda e: e.activation(out=B["vtm"], in_=bC[:, 128:256], func=AF.Copy), reads=[kC], writes=[K("vtm")])
            P.op("dve", lambda e: e.tensor_scalar(out=B["kbg"], in0=B["ktm"], scalar1=col(cbe), scalar2=None, op0=ALU.mult), reads=[K("ktm")] + gate_keys, writes=[K("kbg")])
            P.op("pool", lambda e: e.tensor_scalar(out=B["kd"], in0=B["ktm"], scalar1=col(ekd), scalar2=None, op0=ALU.mult), reads=[K("ktm")] + gate_keys, writes=[K("kd")])
            P.op("dve", lambda e: e.tensor_scalar(out=B["vb"], in0=B["vtm"], scalar1=col(beta), scalar2=None, op0=ALU.mult), reads=[K("vtm")] + gate_keys, writes=[K("vb")])
        yield

        def mmA(e):
            e.matmul(bA[:, 0:128], lhsT=kT, rhs=kT, start=True, stop=True)
            e.matmul(bA[:, 128:256], lhsT=kT, rhs=qT, start=True, stop=True)
            e.matmul(bA[:, 256:384], lhsT=gbc, rhs=tri, start=True, stop=True)
            e.matmul(bA[:, 384:512], lhsT=gbc, rhs=tri, start=True, stop=False)
            return e.matmul(bA[:, 384:512], lhsT=ident, rhs=m_incl, start=False, stop=True)
        P.op("pe", mmA, reads=qk_r + gate_keys + ["cf"], writes=[kA])

        def mmB(e):
            e.matmul(bB[:, 0:128], lhsT=gbc, rhs=tri, start=True, stop=False)
            e.matmul(bB[:, 0:128], lhsT=lbc, rhs=ident, start=False, stop=False)
            e.matmul(bB[:, 0:128], lhsT=ident, rhs=m_str, start=False, stop=True)
            e.matmul(bB[:, 128:256], lhsT=gbc, rhs=tri, start=True, stop=False)
            return e.matmul(bB[:, 128:256], lhsT=ident, rhs=m_pos, start=False, stop=True)
        P.op("pe", mmB, reads=gate_keys + ["cf"], writes=[kB])
        yield
        P.op("act", lambda e: e.activation(out=B["Eq"], in_=bA[:, 256:384], func=AF.Exp), reads=[kA], writes=[K("Eq")])
        P.op("act", lambda e: e.activation(out=B["E1"], in_=bA[:, 384:512], func=AF.Exp, bias=col(negG)), reads=[kA] + gate_keys, writes=[K("E1")])
        P.op("act", lambda e: e.activation(out=B["E2"], in_=bB[:, 0:128], func=AF.Exp, bias=col(negG)), reads=[kB] + gate_keys, writes=[K("E2")])
        P.op("act", lambda e: e.activation(out=B["E3"], in_=bB[:, 128:256], func=AF.Exp, bias=col(Gb), scale=-1.0), reads=[kB] + gate_keys, writes=[K("E3")])
        yield
        if NEUB:
            P.op("dve", lambda e: e.tensor_tensor(out=B["Mb0"], in0=bA[:, 0:128], in1=B["E2"], op=ALU.mult), reads=[kA, K("E2")], writes=[K("Mb0")])
            P.op("dve", lambda e: e.tensor_tensor(out=B["MTb0"], in0=bA[:, 0:128], in1=B["E3"], op=ALU.mult), reads=[kA, K("E3")], writes=[K("MTb0")])
            P.op("dve", lambda e: e.tensor_tensor(out=B["IT"], in0=bA[:, 128:256], in1=B["E1"], op=ALU.mult), reads=[kA, K("E1")], writes=[K("IT")])
            P.op("dve", lambda e: e.tensor_tensor(out=B["Pf"], in0=ident, in1=B["Mb0"], op=ALU.subtract), reads=[K("Mb0"), "cf"], writes=[K("Pf")])
            P.op("act", lambda e: e.activation(out=B["Pb0"], in_=B["Pf"], func=AF.Copy), reads=[K("Pf")], writes=[K("Pb0")])
            P.op("pool", lambda e: e.tensor_tensor(out=B["qd"], in0=qT, in1=B["Eq"], op=ALU.mult), reads=qk_r + [K("Eq")], writes=[K("qd")])
            yield
            for k in range(1, 7):
                pi, ci = (k - 1) % 2, k % 2
                Mp, MTp = B["Mb%d" % pi], B["MTb%d" % pi]
                Mc, MTc = B["Mb%d" % ci], B["MTb%d" % ci]
                kMp, kMTp, kMc, kMTc = K("Mb%d" % pi), K("MTb%d" % pi), K("Mb%d" % ci), K("MTb%d" % ci)
                P.op("pe", lambda e, Mp=Mp, MTp=MTp: e.matmul(bB[:, 256:384], lhsT=Mp, rhs=MTp, start=True, stop=True), reads=[kMp, kMTp], writes=[kB])
                if k < 6:
                    P.op("pe", lambda e, Mp=Mp, MTp=MTp: e.matmul(bB[:, 384:512], lhsT=MTp, rhs=Mp, start=True, stop=True), reads=[kMp, kMTp], writes=[kB])
                P.op("act", lambda e, MTc=MTc: e.activation(out=MTc, in_=bB[:, 256:384], func=AF.Copy), reads=[kB], writes=[kMTc])
                if k < 6:
                    P.op("dve", lambda e, Mc=Mc: e.tensor_copy(out=Mc, in_=bB[:, 384:512]), reads=[kB], writes=[kMc])
                yield
                Pbp, Pbc = B["Pb%d" % pi], B["Pb%d" % ci]
                P.op("pe", lambda e, MTc=MTc, Pbp=Pbp: e.matmul(bB[:, 256:384], lhsT=MTc, rhs=Pbp, start=True, stop=True), reads=[kMTc, K("Pb%d" % pi)], writes=[kB])
                P.op("dve", lambda e: e.tensor_tensor(out=B["Pf"], in0=bB[:, 256:384], in1=B["Pf"], op=ALU.add), reads=[kB, K("Pf")], writes=[K("Pf")])
                if k < 6:
                    P.op("act", lambda e, Pbc=Pbc: e.activation(out=Pbc, in_=B["Pf"], func=AF.Copy), reads=[K("Pf")], writes=[K("Pb%d" % ci)])
                yield
        else:
            P.op("dve", lambda e: e.tensor_tensor(out=B["M0"], in0=bA[:, 0:128], in1=B["E2"], op=ALU.mult), reads=[kA, K("E2")], writes=[K("M0")])
            P.op("dve", lambda e: e.tensor_tensor(out=B["MT0"], in0=bA[:, 0:128], in1=B["E3"], op=ALU.mult), reads=[kA, K("E3")], writes=[K("MT0")])
            P.op("dve", lambda e: e.tensor_tensor(out=B["IT"], in0=bA[:, 128:256], in1=B["E1"], op=ALU.mult), reads=[kA, K("E1")], writes=[K("IT")])
            P.op("dve", lambda e: e.tensor_tensor(out=B["Pf"], in0=ident, in1=B["M0"], op=ALU.subtract), reads=[K("M0"), "cf"], writes=[K("Pf")])
            P.op("pool", lambda e: e.tensor_tensor(out=B["qd"], in0=qT, in1=B["Eq"], op=ALU.mult), reads=qk_r + [K("Eq")], writes=[K("qd")])
            yield
            for k in range(1, 7):
                pi, ci = (k - 1) % 2, k % 2
                Mp, MTp = B["M%d" % pi], B["MT%d" % pi]
                Mc, MTc = B["M%d" % ci], B["MT%d" % ci]
                kMp, kMTp, kMc, kMTc = K("M%d" % pi), K("MT%d" % pi), K("M%d" % ci), K("MT%d" % ci)
                P.op("pe", lambda e, Mp=Mp, MTp=MTp: e.matmul(bB[:, 256:384], lhsT=rr(Mp), rhs=rr(MTp), start=True, stop=True), reads=[kMp, kMTp], writes=[kB])
                if k < 6:
                    P.op("pe", lambda e, Mp=Mp, MTp=MTp: e.matmul(bB[:, 384:512], lhsT=rr(MTp), rhs=rr(Mp), start=True, stop=True), reads=[kMp, kMTp], writes=[kB])
                P.op("act", lambda e, MTc=MTc: e.activation(out=MTc, in_=bB[:, 256:384], func=AF.Copy), reads=[kB], writes=[kMTc])
                if k < 6:
                    P.op("dve", lambda e, Mc=Mc: e.tensor_copy(out=Mc, in_=bB[:, 384:512]), reads=[kB], writes=[kMc])
                yield
                P.op("pe", lambda e, MTc=MTc: e.matmul(bB[:, 256:384], lhsT=rr(MTc), rhs=rr(B["Pf"]), start=True, stop=True), reads=[kMTc, K("Pf")], writes=[kB])
                P.op("dve", lambda e: e.tensor_tensor(out=B["Pf"], in0=bB[:, 256:384], in1=B["Pf"], op=ALU.add), reads=[kB, K("Pf")], writes=[K("Pf")])
                yield
        kPf = K("Pf")
        P.op("pe", lambda e: e.matmul(bB[:, 384:512], lhsT=B["kbg"], rhs=B["Pf"], start=True, stop=True), reads=[K("kbg"), kPf], writes=[kB])
        P.op("dve", lambda e: e.tensor_scalar(out=B["nWT"], in0=bB[:, 384:512], scalar1=-1.0, scalar2=None, op0=ALU.mult), reads=[kB], writes=[K("nWT")])
        yield
        def mmv(e):
            e.matmul(bC[:, 128:256], lhsT=B["Pf"], rhs=B["vb"], start=True, stop=False)
            return e.matmul(bC[:, 128:256], lhsT=B["nWT"], rhs=Sf[d], start=False, stop=True)
        P.op("pe", mmv, reads=[kPf, K("vb"), K("nWT"), ("Sf", d)], writes=[kC])
        P.op("act", lambda e: e.activation(out=B["vnew"], in_=bC[:, 128:256], func=AF.Copy), reads=[kC], writes=[K("vnew")])
        yield

        def mmo(e):
            e.matmul(bC[:, 256:384], lhsT=Sf[d], rhs=B["qd"], start=True, stop=False)
            e.matmul(bC[:, 256:384], lhsT=B["vnew"], rhs=B["IT"], start=False, stop=True)
            return e.matmul(bC[:, 0:128], lhsT=B["kd"], rhs=B["vnew"], start=True, stop=True)
        P.op("pe", mmo, reads=[("Sf", d), K("qd"), K("vnew"), K("IT"), K("kd")], writes=[kC])
        P.op("dve", lambda e: e.scalar_tensor_tensor(out=Sf[d], in0=Sf[d], scalar=col(cdl), in1=bC[:, 0:128], op0=ALU.mult, op1=ALU.add),
             reads=[kC, ("Sf", d)] + gate_keys, writes=[("Sf", d)])
        P.op("dve", lambda e: e.tensor_tensor(out=oacc[:, cs], in0=bC[:, 256:384], in1=oacc[:, cs], op=ALU.add), reads=[kC, ("oacc", c), "oacc"], writes=[("oacc", c)])
        yield

    nsteps = int(_os.environ.get("GSTEPS", "32"))
    stag = int(_os.environ.get("GSTAG", "10"))
    active, nxt, rnd = [], 0, 0
    while nxt < nsteps or active:
        if nxt < nsteps and rnd >= nxt * stag:
            active += [chunk(0, nxt), chunk(1, nxt)]
            nxt += 1
        still = []
        for g_ in active:
            try:
                next(g_)
                still.append(g_)
            except StopIteration:
                pass
        active = still
        rnd += 1

    for tl in range(8):
        s0 = tl * 512
        a_, sb_, r_ = acc[tl % 2], sqb[tl % 2], rs[tl % 2]
        ok = [("oacc", c) for c in range(tl * 4, tl * 4 + 4)] + ["oacc"]
        P.op("act", lambda e, sb_=sb_, s0=s0: e.activation(out=sb_, in_=oacc[:, s0:s0 + 512], func=AF.Square), reads=ok, writes=[("sqb", tl % 2)])
        def mmss(e, sb_=sb_):
            for c4 in range(4):
                r = e.matmul(bank[7][:, c4:c4 + 1], lhsT=sb_[:, c4 * 128:(c4 + 1) * 128], rhs=ones_b[:, 0:1], start=True, stop=True)
            return r
        P.op("pe", mmss, reads=[("sqb", tl % 2), "ones_b"], writes=[("ps", 7)])
        P.op("act", lambda e, r_=r_: e.activation(out=r_[:, 0:4], in_=bank[7][:, 0:4], func=AF.Sqrt, bias=1e-6, scale=1.0 / 128), reads=[("ps", 7)], writes=[("rs", tl % 2)])
        P.op("dve", lambda e, r_=r_: e.reciprocal(out=r_[:, 0:4], in_=r_[:, 0:4]), reads=[("rs", tl % 2)], writes=[("rs", tl % 2)])

        def mmbc(e, r_=r_):
            for c4 in range(4):
                r = e.matmul(bank[6][:, c4 * 128:(c4 + 1) * 128], lhsT=r_[:, c4:c4 + 1].to_broadcast([128, 128]), rhs=ident, start=True, stop=True)
            return r
        P.op("pe", mmbc, reads=[("rs", tl % 2), "cf"], writes=[("ps", 6)])
        P.op("dve", lambda e, a_=a_, s0=s0: e.scalar_tensor_tensor(out=a_, in0=oacc[:, s0:s0 + 512], scalar=gnw_s[:, 0:1], in1=bank[6][:, 0:512], op0=ALU.mult, op1=ALU.mult),
             reads=ok + [("ps", 6), "small"], writes=[("acc", tl % 2)])
        P.op("dve", lambda e, a_=a_, s0=s0: e.tensor_tensor(out=obf[:, s0:s0 + 512], in0=a_, in1=zs[:, s0:s0 + 512], op=ALU.mult),
             reads=[("acc", tl % 2)], writes=[("ofin", tl)])
    P.dma("sp", oT[hi], obf, reads=[("ofin", tl) for tl in range(8)], writes=["oT"], sem_key=("dma", "oT"))


def _att_head(P, A, bank, cb_ident, ones_b, h, hi, qkv, alibi, oT):
    tab = A.alloc([9, 256], F32)
    em = tab
    Oacc = A.alloc([S], F32)
    Dacc = A.alloc([S], F32)
    ex = [A.alloc([256], F32) for _ in range(2)]
    pt = [A.alloc([256], BF16) for _ in range(2)]
    vb = [A.alloc([128], BF16) for _ in range(2)]
    obf = qkv[:, 0, 1024:1024 + S]
    P.dma("sp", tab, alibi[h].rearrange("n p f -> p n f"), writes=["tab"])
    P.op("act", lambda e: e.activation(out=em, in_=tab, func=AF.Exp), reads=["tab"], writes=["em", "tab"])
    P.op("pool", lambda e: e.memset(Oacc, 0.0), writes=["Oacc"])
    P.op("pool", lambda e: e.memset(Dacc, 0.0), writes=["Dacc"])
    qk_all = [("qkv", t, tt) for t in range(3) for tt in range(16)] + ["qkv"]
    it = 0
    for pi, d in enumerate(PATTERNS):
        n = S // d
        nb = n // 128
        for r in range(d):
            for j in range(nb + 1):
                var = 1 if j == 0 else (2 if j == nb else 0)
                q0, q1 = (128, 256) if j == 0 else ((0, 128) if j == nb else (0, 256))
                nq = q1 - q0
                kc0 = 1024 + r + d * (128 * j - 64)
                ks = slice(kc0, kc0 + 127 * d + 1, d)
                qc0 = 1024 + r + d * (128 * j - 128 + q0)
                qs = slice(qc0, qc0 + (nq - 1) * d + 1, d)
                sb_ = it % 2
                bS = bank[sb_]
                bT = bank[2 + sb_]
                P.op("pe", lambda e, bS=bS, ks=ks, qs=qs, nq=nq: e.matmul(bS[:, 0:nq], lhsT=qkv[:, 1, ks], rhs=qkv[:, 0, qs], start=True, stop=True),
                     reads=qk_all, writes=[("ps", sb_)])
                tbv = bT[:, 0:64].bitcast(BF16)
                P.op("pe", lambda e, tbv=tbv, ks=ks: e.transpose(tbv, qkv[:, 2, ks], cb_ident), reads=qk_all + ["cbi"], writes=[("ps", 2 + sb_)])
                P.op("act", lambda e, bS=bS, nq=nq, sb_=sb_: e.activation(out=ex[sb_][:, 0:nq], in_=bS[:, 0:nq], func=AF.Exp), reads=[("ps", sb_)], writes=[("ex", sb_)])
                P.op("dve", lambda e, tbv=tbv, sb_=sb_: e.tensor_copy(out=vb[sb_], in_=tbv), reads=[("ps", 2 + sb_)], writes=[("vb", sb_)])
                P.op("dve", lambda e, sb_=sb_, nq=nq, q0=q0, q1=q1, pi=pi, var=var: e.tensor_tensor(out=pt[sb_][:, 0:nq], in0=ex[sb_][:, 0:nq], in1=em[:, pi * 3 + var, q0:q1], op=ALU.mult),
                     reads=[("ex", sb_), "em"], writes=[("pt", sb_)])
                for half in range(2):
                    lo = half * 128
                    if lo < q0 or lo >= q1:
                        continue
                    tq = j - 1 + half
                    first = (half == 1) or (tq == 0 and False)
                    pb = 4 + (tq % 2)
                    pcol = slice(lo - q0, lo - q0 + 128)
                    is_first = (half == 1)
                    is_last = (half == 0)

                    def mmpv(e, pb=pb, sb_=sb_, pcol=pcol, is_first=is_first, is_last=is_last):
                        e.matmul(bank[pb][:, 0:128], lhsT=vb[sb_], rhs=pt[sb_][:, pcol], start=is_first, stop=is_last)
                        return e.matmul(bank[pb + 2][:, 0:128], lhsT=ones_b, rhs=pt[sb_][:, pcol], start=is_first, stop=is_last)
                    P.op("pe", mmpv, reads=[("vb", sb_), ("pt", sb_), "ones_b"], writes=[("ps", pb), ("ps", pb + 2)])
                    if is_last:
                        oc0 = r + d * 128 * tq
                        osl = slice(oc0, oc0 + 127 * d + 1, d)
                        P.op("dve", lambda e, pb=pb, osl=osl: e.tensor_tensor(out=Oacc[:, osl], in0=bank[pb][:, 0:128], in1=Oacc[:, osl], op=ALU.add),
                             reads=[("ps", pb), "Oacc"], writes=["Oacc"])
                        P.op("dve", lambda e, pb=pb, osl=osl: e.tensor_tensor(out=Dacc[:, osl], in0=bank[pb + 2][:, 0:128], in1=Dacc[:, osl], op=ALU.add),
                             reads=[("ps", pb + 2), "Dacc"], writes=["Dacc"])
                it += 1
    for tl in range(8):
        s0 = tl * 512
        P.op("dve", lambda e, s0=s0: e.reciprocal(out=Dacc[:, s0:s0 + 512], in_=Dacc[:, s0:s0 + 512]), reads=["Dacc"], writes=["Dacc"])
        P.op("dve", lambda e, s0=s0: e.tensor_tensor(out=obf[:, s0:s0 + 512], in0=Oacc[:, s0:s0 + 512], in1=Dacc[:, s0:s0 + 512], op=ALU.mult),
             reads=["Dacc", "Oacc"], writes=["obf", "qkv"])
    P.dma("sp", oT[hi], obf, reads=["obf"], writes=["oT"], sem_key=("dma", "oT"))


def _ptile(w):
    K, N = w.shape
    return np.ascontiguousarray(w.reshape(K // 128, 128, N).transpose(1, 0, 2)).reshape(128, (K // 128) * N)


def l1_inputs(inp, core):
    b, g = core // 4, core % 4
    x = inp["x"][b]
    w_in = inp["w_in"][0]
    hs = [4 * g + i for i in range(4)]
    xT = np.ascontiguousarray(x.T.reshape(32, 128, 16, 256).transpose(2, 1, 0, 3)).reshape(16, 128, 32 * 256)
    wg = np.stack([_ptile(np.concatenate([w_in[:, 0 + h * 128:0 + (h + 1) * 128], w_in[:, 2048 + h * 128:2048 + (h + 1) * 128],
                                          w_in[:, 4096 + h * 128:4096 + (h + 1) * 128], w_in[:, 6144 + h * 128:6144 + (h + 1) * 128]], axis=1)) for h in hs])
    acols = [8192 + d * 16 + h for d in range(2) for h in hs] + [8224 + d * 16 + h for d in range(2) for h in hs]
    wab = _ptile(w_in[:, acols])
    A0 = 8256
    wa = np.stack([_ptile(np.concatenate([w_in[:, A0 + h * 128:A0 + (h + 1) * 128], w_in[:, A0 + 2048 + h * 128:A0 + 2048 + (h + 1) * 128],
                                          w_in[:, A0 + 4096 + h * 128:A0 + 4096 + (h + 1) * 128]], axis=1)) for h in hs])
    gc = inp["gdn_conv"][0]
    gconv = np.zeros((128, 36), np.float32)
    for hi, h in enumerate(hs):
        for t in range(3):
            for j in range(3):
                gconv[:, hi * 9 + t * 3 + j] = gc[j, t * 2048 + h * 128:t * 2048 + (h + 1) * 128]
    sel = lambda a: np.ascontiguousarray(np.broadcast_to(np.array([a[d, h] for d in range(2) for h in hs], np.float32)[None, :], (128, 8)))
    cn = _consts_np()
    return {
        "xT": xT, "anw": np.ascontiguousarray(inp["attn_norm"][0].reshape(32, 128).T), "wg": wg, "wab": wab, "wa": wa,
        "gconv": gconv, "galog": sel(inp["gdn_a_log"][0]), "gdtb": sel(inp["gdn_dt_bias"][0]),
        "gnw": np.ascontiguousarray(inp["gdn_out_norm"][0].reshape(128, 1)),
        "cst": np.stack([cn[n] for n in CONST_NAMES]), "alibi": _alibi_tables(hs),
    }


def run_l1(inputs):
    nc = build_l1(4, 4)
    in_maps = [l1_inputs(inputs, c) for c in range(8)]
    res = run_bass_kernel_spmd(nc, in_maps, core_ids=list(range(8)))
    oT = np.zeros((2, 4096, S), dtype=np.asarray(res.results[0]["oT"]).dtype)
    for c in range(8):
        b, g = c // 4, c % 4
        o = np.asarray(res.results[c]["oT"])
        for i in range(4):
            h = 4 * g + i
            oT[b, h * 128:(h + 1) * 128] = o[i]
            oT[b, 2048 + h * 128:2048 + (h + 1) * 128] = o[4 + i]
    return oT


NW = 514
NI = 512
NFC = 86
FG = 2


def build_l2(n_win=2, stop_after=None):
    nc = bass.Bass("TRN2", target_bir_lowering=False)
    dram = lambda name, shape, dt, kind="ExternalInput": nc.dram_tensor(name, shape, dt, kind=kind).ap()
    xw = dram("xw", [2, 128, 32 * NW], F32)
    ow = dram("ow", [2, 128, 32 * NW], BF16)
    pw = dram("pw", [2, 128, 2 * NI], F32)
    wo = dram("wo", [32, 128, 32 * 128], F32)
    wu = dram("wu", [NFC, 128, 32 * 256], F32)
    wd = dram("wd", [NFC, 128, D], F32)
    wgt = dram("wgt", [32, 128, 32 * 128], F32)
    wp = dram("wp", [128, 2 * D], F32)
    nrm = dram("nrm", [128, 96], F32)
    fcv = dram("fcv", [128, NFC * 6], F32)
    outT = dram("outT", [2, 32, 128, NI], F32, kind="ExternalOutput")

    P = Prog(nc)
    A = Arena(nc, 204 * 1024)
    psg = [nc.psum_tensor("bank%d" % i, [128, 512], F32) for i in range(8)]
    bank = [g.__enter__() for g in psg]

    hT = A.alloc([32, NW], F32)
    uT = A.alloc([32, NW], BF16)
    ones_b = A.alloc([128], BF16)
    nrm_s = A.alloc([96], F32)
    fcv_s = A.alloc([NFC * 6], F32)
    rstd = A.alloc([NW], F32)
    sq = [A.alloc([NW], BF16) for _ in range(2)]
    zero_f = A.alloc([256], F32)
    base_mark = A.mark()
    P.dma("sp", nrm_s, nrm, writes=["consts"])
    P.dma("sp", fcv_s, fcv, writes=["consts"], sem_key=("dma", "consts"))
    P.op("dve", lambda e: e.memset(ones_b, 1.0), writes=["ones_b"])
    P.op("dve", lambda e: e.memset(zero_f, 0.0), writes=["zero_f"])
    hkeys = [("h", c) for c in range(32)]
    ukeys = [("u", c) for c in range(32)]

    def norm(ncol, widx, final_w=None):
        for c in range(32):
            s_ = sq[c % 2]
            P.op("act", lambda e, s_=s_, c=c: e.activation(out=s_, in_=hT[:, c, :], func=AF.Square), reads=[("h", c)], writes=[("sq", c % 2)])

            def mmn(e, s_=s_, c=c):
                e.matmul(bank[6][:, 0:257], lhsT=ones_b, rhs=s_[:, 0:257], start=(c == 0), stop=(c == 31))
                return e.matmul(bank[7][:, 0:257], lhsT=ones_b, rhs=s_[:, 257:514], start=(c == 0), stop=(c == 31))
            P.op("pe", mmn, reads=[("sq", c % 2), "ones_b"], writes=[("ps", 6), ("ps", 7)])
        P.op("act", lambda e: e.activation(out=rstd[:, 0:257], in_=bank[6][:, 0:257], func=AF.Sqrt, bias=1e-6, scale=1.0 / D), reads=[("ps", 6)], writes=["rstd"])
        P.op("act", lambda e: e.activation(out=rstd[:, 257:514], in_=bank[7][:, 0:257], func=AF.Sqrt, bias=1e-6, scale=1.0 / D), reads=[("ps", 7), "rstd"], writes=["rstd"])
        P.op("dve", lambda e: e.reciprocal(out=rstd, in_=rstd), reads=["rstd"], writes=["rstd"])
        for c in range(32):
            wcol = nrm_s[:, ncol * 32 + c:ncol * 32 + c + 1]
            if final_w is None:
                P.op("dve", lambda e, c=c, wcol=wcol: e.scalar_tensor_tensor(out=uT[:, c, :], in0=hT[:, c, :], scalar=wcol, in1=rstd, op0=ALU.mult, op1=ALU.mult),
                     reads=[("h", c), "rstd", "consts"], writes=[("u", c)])
            else:
                ot = otile[c % 2]
                P.op("dve", lambda e, c=c, wcol=wcol, ot=ot: e.scalar_tensor_tensor(out=ot, in0=hT[:, c, 1:513], scalar=wcol, in1=rstd[:, 1:513], op0=ALU.mult, op1=ALU.mult),
                     reads=[("h", c), "rstd", "consts"], writes=[("ot", c % 2)])
                P.dma("sp", outT[final_w, c], ot, reads=[("ot", c % 2)], writes=["outT"], sem_key=("dma", "ot%d" % (c % 2)))

    for w in range(n_win):
        P.barrier()
        A.reset(base_mark)
        oTw = A.alloc([32, NW], BF16)
        wo_t = [A.alloc([32, 128], BF16) for _ in range(3)]
        P.dma("sp", hT, xw[w].rearrange("p (c t) -> p c t", t=NW), writes=hkeys)
        P.dma("sp", oTw, ow[w].rearrange("p (c t) -> p c t", t=NW), writes=["oTw"])
        for dc in range(2):
            P.dma("pool", wo_t[dc % 3], wo[dc].rearrange("p (k n) -> p k n", n=128), writes=[("wo", dc % 3)])
        for dc in range(32):
            if dc + 2 < 32:
                P.dma("pool", wo_t[(dc + 2) % 3], wo[dc + 2].rearrange("p (k n) -> p k n", n=128), writes=[("wo", (dc + 2) % 3)])
            wt = wo_t[dc % 3]
            b0 = 2 * (dc % 2)

            def mmo(e, wt=wt, b0=b0):
                for half in range(2):
                    for k in range(32):
                        r = e.matmul(bank[b0 + half][:, 0:257], lhsT=wt[:, k, :], rhs=oTw[:, k, half * 257:(half + 1) * 257], start=(k == 0), stop=(k == 31))
                return r
            P.op("pe", mmo, reads=[("wo", dc % 3), "oTw"], writes=[("ps", b0), ("ps", b0 + 1)])
            for half in range(2):
                P.op("dve", lambda e, dc=dc, b0=b0, half=half: e.tensor_tensor(out=hT[:, dc, half * 257:(half + 1) * 257], in0=bank[b0 + half][:, 0:257],
                                                                              in1=hT[:, dc, half * 257:(half + 1) * 257], op=ALU.add),
                     reads=[("ps", b0 + half), ("h", dc)], writes=[("h", dc)])
        if stop_after == "A":
            break
        norm(0, w)
        P.barrier()
        A.reset(base_mark)
        wu_t = [A.alloc([32, 256], BF16) for _ in range(2)]
        wd_t = [[A.alloc([D], BF16) for _ in range(FG)] for _ in range(3)]
        act = [[A.alloc([NI], BF16) for _ in range(FG)] for _ in range(2)]
        gcv = [A.alloc([NI], F32) for _ in range(2)]
        ucv = [A.alloc([NI], F32) for _ in range(2)]
        ngrp = NFC // FG

        def load_wu(fc):
            P.dma("pool", wu_t[fc % 2], wu[fc].rearrange("p (k n) -> p k n", n=256), writes=[("wu", fc % 2)])

        def load_wd(g):
            for j in range(FG):
                P.dma("pool", wd_t[g % 3][j], wd[g * FG + j], writes=[("wd", g % 3, j)])

        def up(fc):
            g, j = fc // FG, fc % FG
            wt = wu_t[fc % 2]
            for part, (bb, dstl) in enumerate(((0, gcv), (2, ucv))):
                def mmu(e, wt=wt, part=part, bb=bb):
                    for hh in range(2):
                        for k in range(32):
                            r = e.matmul(bank[bb + hh][:, 0:258], lhsT=wt[:, k, part * 128:(part + 1) * 128], rhs=uT[:, k, hh * 256:hh * 256 + 258],
                                         start=(k == 0), stop=(k == 31))
                    return r
                P.op("pe", mmu, reads=[("wu", fc % 2)] + ukeys, writes=[("ps", bb), ("ps", bb + 1)])
                yield
                dst = dstl[fc % 2]
                dkey = ("cv", part, fc % 2)
                for hh in range(2):
                    o0 = hh * 256
                    cw = lambda jj, part=part, fc=fc: fcv_s[:, fc * 6 + part * 3 + jj:fc * 6 + part * 3 + jj + 1]
                    bk = bank[bb + hh]
                    P.op("dve", lambda e, dst=dst, o0=o0, bk=bk, cw=cw: e.scalar_tensor_tensor(out=dst[:, o0:o0 + 256], in0=bk[:, 0:256], scalar=cw(0), in1=zero_f,
                                                                                             op0=ALU.mult, op1=ALU.add),
                         reads=[("ps", bb + hh), "consts", "zero_f"], writes=[dkey + (hh,)])
                    for jj in (1, 2):
                        P.op("dve", lambda e, dst=dst, o0=o0, bk=bk, cw=cw, jj=jj: e.scalar_tensor_tensor(out=dst[:, o0:o0 + 256], in0=bk[:, jj:jj + 256], scalar=cw(jj),
                                                                                                       in1=dst[:, o0:o0 + 256], op0=ALU.mult, op1=ALU.add),
                             reads=[("ps", bb + hh), "consts", dkey + (hh,)], writes=[dkey + (hh,)])
            gk = [("cv", 0, fc % 2, 0), ("cv", 0, fc % 2, 1)]
            uk = [("cv", 1, fc % 2, 0), ("cv", 1, fc % 2, 1)]
            gd, ud = gcv[fc % 2], ucv[fc % 2]
            P.op("act", lambda e, gd=gd: e.activation(out=gd, in_=gd, func=AF.Silu), reads=gk, writes=gk)
            a_ = act[g % 2][j]
            P.op("dve", lambda e, gd=gd, ud=ud, a_=a_: e.tensor_tensor(out=a_, in0=gd, in1=ud, op=ALU.mult), reads=gk + uk, writes=[("act", g % 2, j)])

        def down(g):
            for dc in range(32):
                if dc % 8 == 0 and dc > 0:
                    yield
                bk = bank[4 + dc % 4]

                def mmd(e, dc=dc, bk=bk, g=g):
                    for j in range(FG):
                        r = e.matmul(bk[:, 0:512], lhsT=wd_t[g % 3][j][:, dc * 128:(dc + 1) * 128], rhs=act[g % 2][j], start=(j == 0), stop=(j == FG - 1))
                    return r
                P.op("pe", mmd, reads=[("wd", g % 3, j) for j in range(FG)] + [("act", g % 2, j) for j in range(FG)], writes=[("ps", 4 + dc % 4)])
                P.op("dve", lambda e, dc=dc, bk=bk: e.tensor_tensor(out=hT[:, dc, 1:513], in0=bk[:, 0:512], in1=hT[:, dc, 1:513], op=ALU.add),
                     reads=[("ps", 4 + dc % 4), ("h", dc)], writes=[("h", dc)])

        load_wu(0)
        load_wu(1)
        load_wd(0)
        def drain(gen):
            for _ in gen:
                pass

        for g in range(ngrp):
            dgen = down(g - 1) if g > 0 else iter(())
            if g + 1 < ngrp:
                load_wd(g + 1)
            for j in range(FG):
                fc = g * FG + j
                for _ in up(fc):
                    next(dgen, None)
                if fc + 2 < NFC:
                    load_wu(fc + 2)
            drain(dgen)
        drain(down(ngrp - 1))
        if stop_after == "B":
            break
        norm(1, w)
        P.barrier()
        A.reset(base_mark)
        wg_t = [A.alloc([32, 128], BF16) for _ in range(3)]
        wp_b = A.alloc([2, D], BF16)
        pT = A.alloc([2, NI], BF16)
        sig = [A.alloc([NI], F32) for _ in range(2)]
        otile = [A.alloc([NI], F32) for _ in range(2)]
        P.dma("pool", wp_b, wp.rearrange("p (k n) -> p k n", n=D), writes=["wp"])
        P.dma("pool", pT, pw[w].rearrange("p (k t) -> p k t", t=NI), writes=["pT"])
        for dc in range(2):
            P.dma("pool", wg_t[dc % 3], wgt[dc].rearrange("p (k n) -> p k n", n=128), writes=[("wg", dc % 3)])
        for dc in range(32):
            if dc + 2 < 32:
                P.dma("pool", wg_t[(dc + 2) % 3], wgt[dc + 2].rearrange("p (k n) -> p k n", n=128), writes=[("wg", (dc + 2) % 3)])
            wt = wg_t[dc % 3]
            bg, bp = bank[dc % 2], bank[2 + dc % 2]

            def mmg(e, wt=wt, bg=bg):
                for k in range(32):
                    r = e.matmul(bg[:, 0:512], lhsT=wt[:, k, :], rhs=uT[:, k, 1:513], start=(k == 0), stop=(k == 31))
                return r
            P.op("pe", mmg, reads=[("wg", dc % 3)] + ukeys, writes=[("ps", dc % 2)])

            def mmp(e, dc=dc, bp=bp):
                for k in range(2):
                    r = e.matmul(bp[:, 0:512], lhsT=wp_b[:, k, dc * 128:(dc + 1) * 128], rhs=pT[:, k, :], start=(k == 0), stop=(k == 1))
                return r
            P.op("pe", mmp, reads=["wp", "pT"], writes=[("ps", 2 + dc % 2)])
            sg = sig[dc % 2]
            P.op("act", lambda e, sg=sg, bg=bg: e.activation(out=sg, in_=bg[:, 0:512], func=AF.Sigmoid), reads=[("ps", dc % 2)], writes=[("sig", dc % 2)])
            P.op("dve", lambda e, sg=sg, bp=bp: e.tensor_tensor(out=sg, in0=sg, in1=bp[:, 0:512], op=ALU.mult), reads=[("ps", 2 + dc % 2), ("sig", dc % 2)], writes=[("sig", dc % 2)])
            P.op("dve", lambda e, sg=sg, dc=dc: e.tensor_tensor(out=hT[:, dc, 1:513], in0=sg, in1=hT[:, dc, 1:513], op=ALU.add), reads=[("sig", dc % 2), ("h", dc)], writes=[("h", dc)])
        norm(2, w, final_w=w)

    P.op("sp", lambda e: e.nop(), reads=["outT"] + hkeys + ukeys)
    P.emit()
    for g in reversed(psg):
        g.__exit__(None, None, None)
    A.close()
    return nc


def l2_weights(inp):
    f = np.float32
    tile_cols = lambda w: np.ascontiguousarray(w.reshape(32, 128, w.shape[1] // 128, 128).transpose(2, 1, 0, 3)).reshape(w.shape[1] // 128, 128, 32 * 128)
    w_up = inp["w_up"][0]
    gt = w_up[:, :11008].reshape(32, 128, NFC, 128)
    ut = w_up[:, 11008:].reshape(32, 128, NFC, 128)
    wu = np.ascontiguousarray(np.concatenate([gt, ut], axis=3).transpose(2, 1, 0, 3)).reshape(NFC, 128, 32 * 256)
    fc = inp["ffn_conv"][0]
    fcv = np.ascontiguousarray(np.stack([fc[:, :11008].reshape(3, NFC, 128), fc[:, 11008:].reshape(3, NFC, 128)], axis=0).transpose(3, 2, 0, 1)).reshape(128, NFC * 6)
    nrm = np.ascontiguousarray(np.concatenate([inp["ffn_norm"][0].reshape(32, 128).T, inp["ple_norm"][0].reshape(32, 128).T,
                                               inp["final_norm"].reshape(32, 128).T], axis=1))
    return {
        "wo": tile_cols(inp["w_out"][0]), "wu": wu, "wd": np.ascontiguousarray(inp["w_down"][0].reshape(NFC, 128, D)),
        "wgt": tile_cols(inp["w_ple_gate"][0]), "wp": _ptile(inp["w_ple_proj"][0]), "nrm": nrm.astype(f), "fcv": fcv.astype(f),
    }


def l2_inputs(inp, oT, core, wts):
    b, tq = core // 4, core % 4
    T0 = 1024 * tq
    x = inp["x"][b]
    p = inp["p"][0, b]
    xw = np.zeros((2, 128, 32, NW), np.float32)
    ow = np.zeros((2, 128, 32, NW), oT.dtype)
    pw = np.zeros((2, 128, 2, NI), np.float32)
    for w in range(2):
        lo = T0 + 512 * w - 1
        a, e = max(lo, 0), min(lo + NW, S)
        xw[w, :, :, a - lo:e - lo] = x[a:e].T.reshape(32, 128, e - a).transpose(1, 0, 2)
        ow[w, :, :, a - lo:e - lo] = oT[b][:, a:e].reshape(32, 128, e - a).transpose(1, 0, 2)
        pw[w] = p[T0 + 512 * w:T0 + 512 * w + NI].T.reshape(2, 128, NI).transpose(1, 0, 2)
    m = dict(wts)
    m.update({"xw": xw.reshape(2, 128, 32 * NW), "ow": ow.reshape(2, 128, 32 * NW), "pw": pw.reshape(2, 128, 2 * NI)})
    return m


def run_l2(inputs, oT):
    nc = build_l2()
    wts = l2_weights(inputs)
    in_maps = [l2_inputs(inputs, oT, c, wts) for c in range(8)]
    res = run_bass_kernel_spmd(nc, in_maps, core_ids=list(range(8)))
    out = np.zeros((2, S, D), np.float32)
    for c in range(8):
        b, tq = c // 4, c % 4
        o = np.asarray(res.results[c]["outT"])
        for w in range(2):
            t0 = 1024 * tq + 512 * w
            out[b, t0:t0 + NI] = o[w].reshape(D, NI).T
    return out


def kernel(**inputs):
    inputs = {k: np.asarray(v) for k, v in inputs.items()}
    oT = run_l1(inputs)
    return run_l2(inputs, oT)
```
